# Optimizing a Trainium2 kernel written in Bass

```python
import jax, jax.numpy as jnp
from jax import lax
import numpy as np

D_MODEL = 1024
BATCH = 2
SEQ = 8192
DEPTH = 1

MIX_WIDTH = D_MODEL
GLA_HEADS = 4
GLA_DK = 64
GLA_DV = 128
GLA_QK = GLA_HEADS * GLA_DK
GLA_V = GLA_HEADS * GLA_DV
GLA_GATE_RANK = 16
GLA_GATE_NORM = 16.0
HGRN_HEADS = 4
HGRN_DF = 128
HGRN_DV = 128
HGRN_F = HGRN_HEADS * HGRN_DF
HGRN_V = HGRN_HEADS * HGRN_DV
D_FF = 4 * D_MODEL
CHUNK = 64
EPS = 1e-6

IN_SPLITS = (GLA_QK, GLA_QK, GLA_V, GLA_V, GLA_GATE_RANK, HGRN_F, HGRN_F, HGRN_V, HGRN_V)
IN_COLS = sum(IN_SPLITS)

kernel_name = "hybrid_gla_hgrn2_sandwich_block"


def _split_points():
    pts, acc = [], 0
    for s in IN_SPLITS[:-1]:
        acc += s
        pts.append(acc)
    return pts


def rmsnorm(x, w):
    xf = x.astype(jnp.float32)
    y = xf * lax.rsqrt(jnp.mean(xf * xf, axis=-1, keepdims=True) + EPS)
    return (y * w.astype(jnp.float32)).astype(x.dtype)


def gated_head_norm(o, gate, w):
    B, T, H, dv = o.shape
    y = o * lax.rsqrt(jnp.mean(o * o, axis=-1, keepdims=True) + EPS)
    y = y.reshape(B, T, H * dv) * w.astype(jnp.float32)
    return (y * jax.nn.silu(gate.astype(jnp.float32))).astype(gate.dtype)


def chunked_gated_linear_attention(q, k, v, g):
    B, T, H, dk = q.shape
    dv = v.shape[-1]
    n = T // CHUNK

    def to_chunks(a):
        return a.astype(jnp.float32).reshape(B, n, CHUNK, H, a.shape[-1]).transpose(1, 0, 3, 2, 4)

    qc, kc, vc, gc = to_chunks(q), to_chunks(k), to_chunks(v), to_chunks(g)
    bc = jnp.cumsum(gc, axis=3)
    causal = jnp.tril(jnp.ones((CHUNK, CHUNK), dtype=bool))[:, :, None]

    def step(S, inp):
        qi, ki, vi, bi = inp
        rel = bi[:, :, :, None, :] - bi[:, :, None, :, :]
        decay = jnp.exp(jnp.where(causal, rel, -jnp.inf))
        scores = jnp.einsum('bhid,bhjd,bhijd->bhij', qi, ki, decay)
        o = jnp.einsum('bhij,bhjv->bhiv', scores, vi) + jnp.einsum('bhid,bhdv->bhiv', qi * jnp.exp(bi), S)
        b_last = bi[:, :, -1:, :]
        S = S * jnp.exp(b_last)[:, :, 0, :, None] + jnp.einsum('bhjd,bhjv->bhdv', ki * jnp.exp(b_last - bi), vi)
        return S, o

    S0 = jnp.zeros((B, H, dk, dv), jnp.float32)
    _, o = lax.scan(step, S0, (qc, kc, vc, bc))
    return o.transpose(1, 0, 3, 2, 4).reshape(B, T, H, dv)


def hybrid_mixer(h, w_in, w_gk_up, b_gk, gla_norm_w, hgrn_norm_w, lb, w_out):
    B, T, _ = h.shape
    z = h @ w_in
    gq, gk, gv, gg, glr, hq, hf, hi, hg = jnp.split(z, _split_points(), axis=-1)

    q_a = (gq * (GLA_DK ** -0.5)).reshape(B, T, GLA_HEADS, GLA_DK)
    k_a = gk.reshape(B, T, GLA_HEADS, GLA_DK)
    v_a = gv.reshape(B, T, GLA_HEADS, GLA_DV)
    log_a = jax.nn.log_sigmoid((glr @ w_gk_up + b_gk).astype(jnp.float32)) / GLA_GATE_NORM
    o_a = chunked_gated_linear_attention(q_a, k_a, v_a, log_a.reshape(B, T, GLA_HEADS, GLA_DK))
    o_a = gated_head_norm(o_a, gg, gla_norm_w)

    f = lb + (1.0 - lb) * jax.nn.sigmoid(hf.astype(jnp.float32))
    q_b = jax.nn.silu(hq.astype(jnp.float32)).reshape(B, T, HGRN_HEADS, HGRN_DF)
    k_b = (1.0 - f).reshape(B, T, HGRN_HEADS, HGRN_DF)
    v_b = hi.reshape(B, T, HGRN_HEADS, HGRN_DV)
    o_b = chunked_gated_linear_attention(q_b, k_b, v_b, jnp.log(f).reshape(B, T, HGRN_HEADS, HGRN_DF))
    o_b = gated_head_norm(o_b, hg, hgrn_norm_w)

    return jnp.concatenate([o_a, o_b], axis=-1) @ w_out


def setup_inputs(seed: int = 0) -> dict:
    key = jax.random.key(seed)
    ks = jax.random.split(key, 16)
    f32 = jnp.float32

    def nrm(k, shape, fan_in):
        return jax.random.normal(k, shape, f32) * (fan_in ** -0.5)

    def gain(k, shape):
        return 1.0 + 0.02 * jax.random.normal(k, shape, f32)

    return {
        "x": jax.random.normal(ks[0], (BATCH, SEQ, D_MODEL), f32),
        "w_in": nrm(ks[1], (DEPTH, D_MODEL, IN_COLS), D_MODEL),
        "w_gk_up": nrm(ks[2], (DEPTH, GLA_GATE_RANK, GLA_QK), GLA_GATE_RANK),
        "b_gk": 0.02 * jax.random.normal(ks[3], (DEPTH, GLA_QK), f32),
        "gla_norm_w": gain(ks[4], (DEPTH, GLA_V)),
        "hgrn_norm_w": gain(ks[5], (DEPTH, HGRN_V)),
        "hgrn_lower_bounds": 0.01 * jax.random.normal(ks[6], (DEPTH + 1, HGRN_F), f32),
        "w_out": nrm(ks[7], (DEPTH, MIX_WIDTH, D_MODEL), MIX_WIDTH),
        "pre_mix_norm": gain(ks[8], (DEPTH, D_MODEL)),
        "post_mix_norm": gain(ks[9], (DEPTH, D_MODEL)),
        "pre_mlp_norm": gain(ks[10], (DEPTH, D_MODEL)),
        "post_mlp_norm": gain(ks[11], (DEPTH, D_MODEL)),
        "w_up": nrm(ks[12], (DEPTH, D_MODEL, D_FF), D_MODEL),
        "w_down": nrm(ks[13], (DEPTH, D_FF, D_MODEL), D_FF),
    }


def reference(x, w_in, w_gk_up, b_gk, gla_norm_w, hgrn_norm_w, hgrn_lower_bounds, w_out,
              pre_mix_norm, post_mix_norm, pre_mlp_norm, post_mlp_norm, w_up, w_down):
    lbs = jnp.cumsum(jax.nn.softmax(hgrn_lower_bounds.astype(jnp.float32), axis=0), axis=0)
    h = x
    for l in range(DEPTH):
        mix = hybrid_mixer(rmsnorm(h, pre_mix_norm[l]), w_in[l], w_gk_up[l], b_gk[l],
                           gla_norm_w[l], hgrn_norm_w[l], lbs[l], w_out[l])
        h = h + rmsnorm(mix, post_mix_norm[l])
        u = rmsnorm(h, pre_mlp_norm[l]) @ w_up[l]
        m = jnp.square(jax.nn.relu(u)) @ w_down[l]
        h = h + rmsnorm(m, post_mlp_norm[l])
    return h
```

```python
import contextlib
import numpy as np
import concourse.bass as bass
import concourse.mybir as mybir
from concourse.bass_utils import run_bass_kernel_spmd

F32 = mybir.dt.float32
BF16 = mybir.dt.bfloat16
ALU = mybir.AluOpType
AF = mybir.ActivationFunctionType

D = 1024
SEQ = 8192
TOK = 2048
GT = 512
NG_MIX = SEQ // GT
NG_FFN = TOK // GT
DFF = 4096
EPS = 1e-6
CH = 128
NFM = 656


class Ev:
    __slots__ = ("sem", "val", "seq", "eng", "used", "dma")

    def __init__(self, sem, seq, eng, dma=False):
        self.sem, self.seq, self.eng, self.dma = sem, seq, eng, dma
        self.val = None
        self.used = dma


class Res:
    __slots__ = ("name", "ws", "rs", "dsem", "dseq")

    def __init__(self, name):
        self.name = name
        self.ws, self.rs = [], []
        self.dsem, self.dseq = None, 0


class Prog:
    ENGS = ("pe", "act", "dve", "pool", "sp")

    def __init__(self, nc):
        self.nc = nc
        self.q = {e: [] for e in self.ENGS}
        self.sem = {e: nc.alloc_semaphore("prog_" + e) for e in self.ENGS}
        self.seq = {e: 0 for e in self.ENGS}
        self.ccsem = nc.alloc_semaphore("ccsem")
        self.ccseq = 0
        self.cnt = {e: 0 for e in self.ENGS}

    def _deps(self, eng, reads, writes):
        waits = []
        for r in reads:
            for ev in r.ws:
                if ev.eng == eng and eng == "pe":
                    continue
                waits.append(ev)
        for w in writes:
            for ev in w.rs + w.ws:
                if ev.eng == eng:
                    continue
                waits.append(ev)
        return waits

    @staticmethod
    def _commit(ev, reads, writes):
        for r in reads:
            r.rs.append(ev)
        for w in writes:
            if w.rs:
                w.ws, w.rs = [ev], []
            else:
                w.ws.append(ev)
                if len(w.ws) > 24:
                    w.ws = w.ws[-24:]

    def op(self, eng, fn, reads=(), writes=()):
        waits = self._deps(eng, reads, writes)
        self.seq[eng] += 1
        ev = Ev(self.sem[eng], self.seq[eng], eng)
        self.q[eng].append([waits, fn, ev, 1])
        self._commit(ev, reads, writes)
        return ev

    def dma(self, eng, fn, primary, reads=(), writes=(), n=1):
        waits = self._deps(eng, reads, writes)
        if primary.dsem is None:
            primary.dsem = self.nc.alloc_semaphore("d_" + primary.name)
        primary.dseq += n
        ev = Ev(primary.dsem, primary.dseq, None, dma=True)
        ev.val = 16 * primary.dseq
        self.q[eng].append([waits, fn, ev, 16])
        self._commit(ev, reads, writes)
        return ev

    def collective(self, fn, reads=(), writes=()):
        waits = self._deps("pool", reads, writes)
        self.ccseq += 1
        ev = Ev(self.ccsem, self.ccseq, None, dma=True)
        ev.val = self.ccseq
        self.q["pool"].append([waits, fn, ev, 0])
        self._commit(ev, reads, writes)
        return ev

    def wait_all(self, eng, evs):
        self.q[eng].append([list(evs), None, None, 0])

    def finalize(self):
        for eng in self.ENGS:
            seen = {}
            for item in self.q[eng]:
                kept = []
                for ev in item[0]:
                    k = id(ev.sem)
                    if seen.get(k, 0) >= ev.seq:
                        continue
                    seen[k] = ev.seq
                    ev.used = True
                    kept.append(ev)
                item[0] = kept
        for eng in self.ENGS:
            for item in self.q[eng]:
                ev = item[2]
                if ev is not None and not ev.dma and ev.used:
                    self.cnt[eng] += 1
                    ev.val = self.cnt[eng]

    def run(self, eng, e):
        for waits, fn, ev, kind in self.q[eng]:
            for w in waits:
                e.wait_ge(w.sem, w.val)
            if fn is None:
                continue
            if kind == 16:
                fn(e, lambda ins, s=ev.sem: ins.then_inc(s, 16))
            elif kind == 0:
                fn(e).then_inc(ev.sem)
            else:
                ins = fn(e)
                if ev.used:
                    ins.then_inc(ev.sem, 1)
        self.q[eng] = []


def build_nc():
    nc = bass.Bass("TRN2", target_bir_lowering=False)

    def din(name, shape, dt=F32):
        return nc.dram_tensor(name, shape, dt, kind="ExternalInput").ap()

    xb = din("xb", [SEQ, D])
    xo = din("xo", [TOK, D])
    wfm = din("wfm", [D, NFM])
    wtm = din("wtm", [D, 256])
    wgk = din("wgk", [17, 64])
    lbp = din("lbp", [128, 2])
    gnw = din("gnw", [128, 2])
    nrm = din("nrm", [4, D])
    w_out = din("w_out", [D, D])
    w_up = din("w_up", [D, DFF])
    w_down = din("w_down", [DFF, D])
    cmask_in = din("cmask", [128, 256])
    ident_in = din("ident", [128, 128])
    out = nc.dram_tensor("out", [TOK, D], F32, kind="ExternalOutput").ap()

    ybounce = nc.dram_tensor("ybounce", [4, 256, TOK], BF16).ap()
    agout = nc.dram_tensor("agout", [4, 1024, TOK], BF16).ap()
    wupb = nc.dram_tensor("wupb", [D, DFF], BF16).ap()

    P = Prog(nc)
    es_outer = contextlib.ExitStack()

    def sb(es, name, shape, dt=F32):
        return es.enter_context(nc.sbuf_tensor(name, shape, dt))

    def ps(es, name, shape, dt=F32):
        return es.enter_context(nc.psum_tensor(name, shape, dt))

    with es_outer:
        w_out_sb = sb(es_outer, "w_out_sb", [128, 8, D], BF16)
        w_down_sb = sb(es_outer, "w_down_sb", [128, 32, D], BF16)
        ident = sb(es_outer, "ident_sb", [128, 128], BF16)
        eps_t = sb(es_outer, "eps_t", [128, 1])
        one_t = sb(es_outer, "one_t", [128, 1])
        R_wout, R_wdown, R_ident, R_const = Res("wout"), Res("wdown"), Res("ident"), Res("const")
        R_wupb = Res("wupb")
        yin0 = sb(es_outer, "yin0", [128, 8, GT], BF16)
        R_yin0 = Res("yin0")
        R_yb = [Res("yb%d" % q) for q in range(4)]
        R_ag = [Res("ag%d" % q) for q in range(4)]

        with contextlib.ExitStack() as es:
            wfm_sb = sb(es, "wfm_sb", [128, 8, NFM], BF16)
            wtm_sb = sb(es, "wtm_sb", [128, 8, 256], BF16)
            wgk_sb = sb(es, "wgk_sb", [17, 64])
            lbt = sb(es, "lbt", [128, 2])
            lbw = sb(es, "lbw", [128, 8])
            gnw_sb = sb(es, "gnw_sb", [128, 2])
            wbc0 = sb(es, "wbc0", [128, D])
            scanmask = sb(es, "scanmask", [128, GT])
            cmask = sb(es, "cmask_sb", [128, 256])
            ones_bf = sb(es, "ones_bf", [128, 128], BF16)
            NX = 4
            xt = [sb(es, "xt%d" % i, [128, D]) for i in range(NX)]
            stat = [sb(es, "stat%d" % i, [128, 4]) for i in range(NX)]
            hnb = [sb(es, "hnb%d" % i, [128, D], BF16) for i in range(2)]
            hnT = [sb(es, "hnT%d" % i, [128, 8, GT], BF16) for i in range(2)]
            glrT = sb(es, "glrT", [17, GT])
            tA = [sb(es, "tA%d" % i, [128, GT]) for i in range(4)]
            tB = [sb(es, "tB%d" % i, [128, GT]) for i in range(6)]
            tC = [sb(es, "tC%d" % i, [128, GT]) for i in range(1)]
            sg = [[sb(es, "sg%d_%d" % (i, j), [128, GT]) for j in range(2)] for i in range(2)]
            tD = [sb(es, "tD%d" % i, [128, GT]) for i in range(3)]
            osq = sb(es, "osq", [128, GT], BF16)
            dec = [sb(es, "dec%d" % i, [128, 16]) for i in range(2)]
            qk = [[sb(es, "qk%d_%d" % (i, j), [128, GT], BF16) for j in range(4)] for i in range(2)]
            v_sb = [sb(es, "v_sb%d" % i, [128, 4, 256], BF16) for i in range(2)]
            kd_tok = [sb(es, "kd_tok%d" % i, [128, 192], BF16) for i in range(2)]
            scm = [sb(es, "scm%d" % i, [128, 256], BF16) for i in range(2)]
            S32 = sb(es, "S32", [128, 256])
            Sb = [sb(es, "Sb%d" % i, [128, 256], BF16) for i in range(4)]
            tmpS = sb(es, "tmpS", [128, 4, 128])
            yT = [sb(es, "yT%d" % i, [128, 2, GT], BF16) for i in range(1)]

            p_tr = ps(es, "p_tr", [128, 8, 128], BF16)
            p_fm = [ps(es, "p_fm%d" % i, [128, GT]) for i in range(2)]
            p_ks = ps(es, "p_ks", [128, GT])
            p_kt = p_ks[:, 256:512].bitcast(BF16)
            p_vg = ps(es, "p_vg", [128, GT])
            p_ot = [ps(es, "p_ot%d" % i, [128, GT]) for i in range(2)]
            p_st = ps(es, "p_st", [128, 4, 128])

            R = {}

            def res(name):
                if name not in R:
                    R[name] = Res(name)
                return R[name]

            def ld_small(e, then):
                then(e.dma_start(out=wgk_sb[:], in_=wgk))
                then(e.dma_start(out=lbt[:], in_=lbp))
                then(e.dma_start(out=gnw_sb[:], in_=gnw))
                then(e.dma_start(out=cmask[:], in_=cmask_in))
                then(e.dma_start(out=wbc0[:], in_=nrm[0:1, :].partition_broadcast(128)))
            P.dma("sp", ld_small, res("small"), writes=[res("small")], n=5)
            P.dma("pool", lambda e, then: then(e.dma_start(out=ident[:], in_=ident_in)),
                  R_ident, writes=[R_ident])
            P.dma("pool", lambda e, then: then(e.dma_start(
                out=wfm_sb[:], in_=wfm.rearrange("(c p) n -> p c n", p=128))),
                res("wfm"), writes=[res("wfm")])
            P.dma("pool", lambda e, then: then(e.dma_start(
                out=wtm_sb[:], in_=wtm.rearrange("(c p) n -> p c n", p=128))),
                res("wtm"), writes=[res("wtm")])

            def consts(e):
                e.memset(eps_t[:], EPS)
                e.memset(one_t[:], 1.0)
                e.memset(scanmask[:].rearrange("p (c k) -> p c k", k=CH)[:, :, 1:CH], 1.0)
                e.memset(scanmask[:].rearrange("p (c k) -> p c k", k=CH)[:, :, 0:1], 0.0)
                e.memset(ones_bf[:], 1.0)
                e.memset(glrT[:], 1.0)
                e.memset(S32[:], 0.0)
                e.memset(Sb[0][:], 0.0)
                e.memset(Sb[1][:], 0.0)
                e.memset(Sb[2][:], 0.0)
                return e.memset(Sb[3][:], 0.0)
            P.op("dve", consts, writes=[R_const, res("glrT"), res("S32"), res("Sb0"), res("Sb1"),
                                        res("Sb2"), res("Sb3")])

            P.op("dve", lambda e: e.tensor_sub(out=lbw[:, 0:1], in0=lbt[:, 1:2], in1=lbt[:, 0:1]),
                 reads=[res("small")], writes=[res("lbw0")])
            P.op("act", lambda e: e.activation(out=lbw[:, 1:2], in_=lbw[:, 0:1], func=AF.Exp),
                 reads=[res("lbw0")], writes=[res("lbw1")])
            P.op("dve", lambda e: e.tensor_scalar_add(out=lbw[:, 2:3], in0=lbw[:, 1:2], scalar1=1.0),
                 reads=[res("lbw1")], writes=[res("lbw2")])
            P.op("dve", lambda e: e.reciprocal(out=lbw[:, 2:3], in_=lbw[:, 2:3]),
                 reads=[res("lbw2")], writes=[res("lbw2")])
            P.op("dve", lambda e: e.tensor_mul(out=lbw[:, 3:4], in0=lbw[:, 1:2], in1=lbw[:, 2:3]),
                 reads=[res("lbw1"), res("lbw2")], writes=[res("omlb")])
            omlb = lbw[:, 3:4]
            P.op("dve", lambda e: e.tensor_scalar_mul(out=lbw[:, 4:5], in0=lbw[:, 3:4], scalar1=-1.0),
                 reads=[res("omlb")], writes=[res("omlb")])
            nomlb = lbw[:, 4:5]

            def big_loads(G):
                if G == 0:
                    P.dma("pool", lambda e, then: then(e.dma_start(
                        out=w_out_sb[:], in_=w_out.rearrange("(c p) n -> p c n", p=128))),
                        R_wout, writes=[R_wout])
                elif 1 <= G <= 4:
                    i = G - 1
                    P.dma("pool", lambda e, then, i=i: then(e.dma_start(
                        out=w_down_sb[:, 8 * i:8 * i + 8, :],
                        in_=w_down[1024 * i:1024 * (i + 1), :].rearrange("(c p) n -> p c n", p=128))),
                        R_wdown, writes=[R_wdown])
                elif 5 <= G <= 12:
                    i = G - 5
                    P.dma("pool", lambda e, then, i=i: then(e.dma_start(
                        out=wupb[128 * i:128 * (i + 1), :], in_=w_up[128 * i:128 * (i + 1), :])),
                        R_wupb, writes=[R_wupb])

            fm_rr = [0]

            fm_pend = [False, False]

            def fm_bank(hold=False):
                i = fm_rr[0] % 2
                if fm_pend[i]:
                    i = 1 - i
                assert not fm_pend[i]
                fm_rr[0] = i + 1
                fm_pend[i] = hold
                return p_fm[i], res("p_fm%d" % i)

            def fm_release(bank):
                for i in range(2):
                    if bank is p_fm[i]:
                        fm_pend[i] = False

            NXP = 3
            PE_AHEAD = True
            DEFER_AT = [0, 2, 5]

            def x_load(n):
                if n >= NG_MIX * 4:
                    return
                xs = n % NX
                tok0 = n * 128
                Rx = res("xt%d" % xs)
                P.dma("sp", lambda e, then: then(
                    e.dma_start(out=xt[xs][:], in_=xb[tok0:tok0 + 128, :])), Rx, writes=[Rx])

            def x_pieces(G):
                h3 = G % 2
                pcs = []
                for t in range(4):
                    n = G * 4 + t
                    xs = n % NX
                    hb = n % 2
                    Rx, Rst, Rhb = res("xt%d" % xs), res("stat%d" % xs), res("hnb%d" % hb)

                    def ca(n=n, xs=xs, hb=hb, Rx=Rx, Rst=Rst, Rhb=Rhb):
                        P.op("act", lambda e: e.activation(
                            out=hnb[hb][:], in_=xt[xs][:], func=AF.Square, accum_out=stat[xs][:, 0:1]),
                            reads=[Rx], writes=[Rhb, Rst])
                        P.op("act", lambda e: e.activation(
                            out=stat[xs][:, 1:2], in_=stat[xs][:, 0:1], func=AF.Ln,
                            scale=1.0 / D, bias=eps_t[:, 0:1]),
                            reads=[Rst, R_const], writes=[Rst])
                        P.op("act", lambda e: e.activation(
                            out=stat[xs][:, 2:3], in_=stat[xs][:, 1:2], func=AF.Exp, scale=-0.5),
                            reads=[Rst], writes=[Rst])
                        P.op("dve", lambda e: e.scalar_tensor_tensor(
                            out=hnb[hb][:], in0=xt[xs][:], scalar=stat[xs][:, 2:3], in1=wbc0[:],
                            op0=ALU.mult, op1=ALU.mult),
                            reads=[Rx, Rst, res("small")], writes=[Rhb])
                        if not EARLY_X:
                            x_load(n + NXP)

                    def cb(t=t, hb=hb, Rhb=Rhb):
                        def tr(e):
                            ins = None
                            for c in range(8):
                                ins = e.transpose(out=p_tr[:, c, :], in_=hnb[hb][:, c * 128:(c + 1) * 128],
                                                  identity=ident[:])
                            return ins
                        P.op("pe", tr, reads=[Rhb, R_ident], writes=[res("p_tr")])
                        P.op("dve", lambda e: e.tensor_copy(
                            out=hnT[h3][:, :, t * 128:(t + 1) * 128], in_=p_tr[:]),
                            reads=[res("p_tr")], writes=[res("hnT%d" % h3)])
                    pcs += [ca, cb]
                return pcs

            def proj_fm(G, col0, M):
                h3 = G % 2
                bank, Rb = fm_bank(hold=True)

                def mm(e):
                    ins = None
                    for c in range(8):
                        ins = e.matmul(bank[0:M, :], lhsT=wfm_sb[:, c, col0:col0 + M],
                                       rhs=hnT[h3][:, c, :], start=(c == 0), stop=(c == 7))
                    return ins
                P.op("pe", mm, reads=[res("hnT%d" % h3), res("wfm")], writes=[Rb])
                return bank, Rb

            def e_pieces(G):
                gs = G % 2
                h3 = G % 2
                Rh = res("hnT%d" % h3)
                q_a, k_a, q_b, k_b = qk[gs]
                Rqk = res("qk%d" % gs)
                Rdec = res("dec%d" % gs)
                pcs = []
                for t in range(4):
                    def vpe(t=t):
                        def mmv(e):
                            ins = None
                            for c in range(8):
                                ins = e.matmul(p_vg[:, 0:256], lhsT=hnT[h3][:, c, t * 128:(t + 1) * 128],
                                               rhs=wtm_sb[:, c, :], start=(c == 0), stop=(c == 7))
                            return ins
                        P.op("pe", mmv, reads=[Rh, res("wtm")], writes=[res("p_vg")])

                    def vrest(t=t):
                        P.op("dve", lambda e: e.tensor_copy(out=v_sb[gs][:, t, :], in_=p_vg[:, 0:256]),
                             reads=[res("p_vg")], writes=[res("v_sb%d" % gs)])
                    pcs.append((None, (lambda vpe=vpe, vrest=vrest: (vpe(), vrest()))))

                st = {}

                def mk_proj(key, col0, M):
                    def f():
                        st[key] = proj_fm(G, col0, M)
                    return f

                def glr_rest():
                    bank, Rb = st["glr"]
                    fm_release(bank)
                    P.op("act", lambda e: e.copy(out=glrT[0:16, :], in_=bank[0:16, :]),
                         reads=[Rb], writes=[res("glrT")])
                    bank2, Rb2 = fm_bank()
                    st["xg"] = (bank2, Rb2)
                    P.op("pe", lambda e: e.matmul(
                        bank2[0:64, :], lhsT=wgk_sb[0:17, :], rhs=glrT[0:17, :], start=True, stop=True),
                        reads=[res("glrT"), res("small")], writes=[Rb2])
                pcs = [(mk_proj("glr", 640, 16), (lambda: None)), pcs[0], pcs[1], (None, glr_rest), pcs[2], pcs[3]]

                def gla_rest():
                    bank, Rb = st["xg"]
                    fm_release(bank)
                    P.op("act", lambda e: e.activation(
                        out=tA[0][0:64, :], in_=bank[0:64, :], func=AF.Exp, scale=-1.0),
                        reads=[Rb], writes=[res("tA0")])
                    P.op("act", lambda e: e.activation(
                        out=tA[0][0:64, :], in_=tA[0][0:64, :], func=AF.Ln, bias=one_t[0:64, 0:1]),
                        reads=[res("tA0"), R_const], writes=[res("tA0")])
                    P.op("dve", lambda e: e.tensor_tensor_scan(
                        out=tA[1][0:64, :], data0=scanmask[0:64, :], data1=tA[0][0:64, :], initial=0.0,
                        op0=ALU.mult, op1=ALU.add),
                        reads=[res("tA0"), R_const], writes=[res("tA1")])
                    P.op("act", lambda e: e.activation(
                        out=tA[2][0:64, :], in_=tA[1][0:64, :], func=AF.Exp, scale=-1.0 / 16),
                        reads=[res("tA1")], writes=[res("tA2")])
                    P.op("act", lambda e: e.activation(
                        out=tA[3][0:64, :], in_=tA[1][0:64, :], func=AF.Exp, scale=1.0 / 16),
                        reads=[res("tA1")], writes=[res("tA3")])
                    P.op("dve", lambda e: e.tensor_copy(
                        out=dec[gs][0:64, 0:4],
                        in_=tA[2][0:64, :].rearrange("p (c k) -> p c k", k=CH)[:, :, CH - 1]),
                        reads=[res("tA2")], writes=[Rdec])
                pcs.append((None, gla_rest))

                def qa_rest():
                    bank, Rb = st["qa"]
                    fm_release(bank)
                    P.op("dve", lambda e: e.scalar_tensor_tensor(
                        out=q_a[0:64, :], in0=bank[0:64, :], scalar=0.125, in1=tA[2][0:64, :],
                        op0=ALU.mult, op1=ALU.mult),
                        reads=[Rb, res("tA2")], writes=[Rqk])
                pcs.append((mk_proj("qa", 0, 64), qa_rest))

                def ka_rest():
                    bank, Rb = st["ka"]
                    fm_release(bank)
                    P.op("dve", lambda e: e.tensor_mul(
                        out=k_a[0:64, :], in0=bank[0:64, :], in1=tA[3][0:64, :]),
                        reads=[Rb, res("tA3")], writes=[Rqk])
                pcs.append((mk_proj("ka", 64, 64), ka_rest))

                def f1_rest():
                    bank, Rb = st["f"]
                    fm_release(bank)
                    P.op("act", lambda e: e.activation(out=tB[0][:], in_=bank[:], func=AF.Exp),
                         reads=[Rb], writes=[res("tB0")])
                    P.op("act", lambda e: e.activation(
                        out=tB[0][:], in_=tB[0][:], func=AF.Ln, bias=one_t[:, 0:1]),
                        reads=[res("tB0"), R_const], writes=[res("tB0")])
                    P.op("act", lambda e: e.activation(
                        out=tB[2][:], in_=tB[0][:], func=AF.Exp, scale=-1.0),
                        reads=[res("tB0")], writes=[res("tB2")])
                    P.op("act", lambda e: e.activation(
                        out=tB[3][:], in_=tB[2][:], func=AF.Ln, scale=nomlb, bias=one_t[:, 0:1]),
                        reads=[res("tB2"), R_const, res("omlb")], writes=[res("tB3")])
                pcs.append((mk_proj("f", 384, 128), f1_rest))

                def f2_rest():
                    P.op("dve", lambda e: e.tensor_tensor_scan(
                        out=tB[0][:], data0=scanmask[:], data1=tB[3][:], initial=0.0,
                        op0=ALU.mult, op1=ALU.add),
                        reads=[res("tB3"), R_const], writes=[res("tB0")])
                    P.op("act", lambda e: e.activation(out=tB[1][:], in_=tB[0][:], func=AF.Exp, scale=-1.0),
                         reads=[res("tB0")], writes=[res("tB1")])
                    P.op("act", lambda e: e.activation(out=tB[4][:], in_=tB[0][:], func=AF.Exp),
                         reads=[res("tB0")], writes=[res("tB4")])
                    P.op("dve", lambda e: e.tensor_copy(
                        out=dec[gs][:, 8:12],
                        in_=tB[4][:].rearrange("p (c k) -> p c k", k=CH)[:, :, CH - 1]),
                        reads=[res("tB4")], writes=[Rdec])
                    P.op("dve", lambda e: e.scalar_tensor_tensor(
                        out=k_b[:], in0=tB[2][:], scalar=omlb, in1=tB[1][:], op0=ALU.mult, op1=ALU.mult),
                        reads=[res("tB2"), res("tB1"), res("omlb")], writes=[Rqk])
                pcs.append((None, f2_rest))

                def q_rest():
                    bank, Rb = st["q"]
                    fm_release(bank)
                    P.op("act", lambda e: e.activation(out=tB[5][:], in_=bank[:], func=AF.Exp, scale=-1.0),
                         reads=[Rb], writes=[res("tB5")])
                    P.op("act", lambda e: e.activation(
                        out=tB[5][:], in_=tB[5][:], func=AF.Ln, bias=one_t[:, 0:1]),
                        reads=[res("tB5"), R_const], writes=[res("tB5")])
                    P.op("act", lambda e: e.activation(
                        out=tB[5][:], in_=tB[5][:], func=AF.Exp, scale=-1.0),
                        reads=[res("tB5")], writes=[res("tB5")])
                    P.op("dve", lambda e: e.tensor_mul(out=tB[3][:], in0=bank[:], in1=tB[5][:]),
                         reads=[Rb, res("tB5")], writes=[res("tB3")])
                    P.op("dve", lambda e: e.tensor_mul(out=q_b[:], in0=tB[3][:], in1=tB[4][:]),
                         reads=[res("tB3"), res("tB4")], writes=[Rqk])
                pcs.append((mk_proj("q", 256, 128), q_rest))

                for h, col0 in ((0, 128), (1, 512)):
                    def g_rest(h=h):
                        bank, Rb = st["g%d" % h]
                        fm_release(bank)
                        Rc = res("tC0")
                        P.op("act", lambda e: e.activation(out=tC[0][:], in_=bank[:], func=AF.Exp, scale=-1.0),
                             reads=[Rb], writes=[Rc])
                        P.op("act", lambda e: e.activation(
                            out=tC[0][:], in_=tC[0][:], func=AF.Ln, bias=one_t[:, 0:1]),
                            reads=[Rc, R_const], writes=[Rc])
                        P.op("act", lambda e: e.activation(
                            out=tC[0][:], in_=tC[0][:], func=AF.Exp, scale=-1.0),
                            reads=[Rc], writes=[Rc])
                        P.op("dve", lambda e: e.tensor_mul(out=sg[gs][h][:], in0=bank[:], in1=tC[0][:]),
                             reads=[Rb, Rc], writes=[res("sg%d_%d" % (gs, h))])
                    pcs.append((mk_proj("g%d" % h, col0, 128), g_rest))
                return pcs

            def pipeline_b(pcs):
                outl = []
                n = len(pcs)
                if not PE_AHEAD:
                    for i in range(n):
                        def f0(i=i):
                            if pcs[i][0] is not None:
                                pcs[i][0]()
                            pcs[i][1]()
                        outl.append(f0)
                    return outl
                for i in range(n + 1):
                    def f(i=i):
                        if i < n and pcs[i][0] is not None:
                            pcs[i][0]()
                        if i >= 1:
                            pcs[i - 1][1]()
                    outl.append(f)
                return outl

            sb_idx = [0]
            NSB = 4

            def a_pieces(G):
                gs = G % 2
                q_a, k_a, q_b, k_b = qk[gs]
                Rqk, Rdec, Rv = res("qk%d" % gs), res("dec%d" % gs), res("v_sb%d" % gs)
                pcs = []
                for t in range(4):
                    n = G * 4 + t
                    c0, c1 = t * 128, (t + 1) * 128
                    ks = n % 2
                    Rkd, Rscm = res("kd_tok%d" % ks), res("scm%d" % ks)

                    def pre(t=t, c0=c0, c1=c1, ks=ks, Rkd=Rkd, Rscm=Rscm):
                        def trk(e):
                            e.transpose(out=p_kt[:, 0:64], in_=k_a[0:64, c0:c1], identity=ident[0:64, 0:64])
                            return e.transpose(out=p_kt[:, 64:192], in_=k_b[:, c0:c1], identity=ident[:])
                        P.op("pe", trk, reads=[Rqk, R_ident], writes=[res("p_ks")])

                        def sc(e):
                            e.matmul(p_ks[:, 0:128], lhsT=k_a[0:64, c0:c1], rhs=q_a[0:64, c0:c1],
                                     start=True, stop=True)
                            return e.matmul(p_ks[:, 128:256], lhsT=k_b[:, c0:c1], rhs=q_b[:, c0:c1],
                                            start=True, stop=True)
                        P.op("pe", sc, reads=[Rqk], writes=[res("p_ks")])

                        def cpk(e):
                            return e.tensor_copy(out=kd_tok[ks][:, 0:192], in_=p_kt[:, 0:192])
                        P.op("dve", cpk, reads=[res("p_ks")], writes=[Rkd])
                        P.op("dve", lambda e: e.tensor_mul(
                            out=scm[ks][:], in0=p_ks[:, 0:256], in1=cmask[:]),
                            reads=[res("p_ks"), res("small")], writes=[Rscm])

                    def pre2(t=t, ks=ks, Rkd=Rkd):
                        def ds(e):
                            e.matmul(p_st[0:64, 0, :], lhsT=kd_tok[ks][:, 0:64],
                                     rhs=v_sb[gs][:, t, 0:128], start=True, stop=True)
                            return e.matmul(p_st[:, 1, :], lhsT=kd_tok[ks][:, 64:192],
                                            rhs=v_sb[gs][:, t, 128:256], start=True, stop=True)
                        P.op("pe", ds, reads=[Rkd, Rv], writes=[res("p_st")])
                    tile_pcs = [pre, pre2]

                    def chunk(t=t, c0=c0, c1=c1, ks=ks, Rscm=Rscm):
                        si = sb_idx[0] % NSB
                        so = (si + 1) % NSB
                        sb_idx[0] += 1
                        sl = t % 2
                        Rtmp = res("tmpS%d" % sl)

                        def tmp(e):
                            e.activation(out=tmpS[0:64, sl, :], in_=p_st[0:64, 0, :], func=AF.Identity,
                                         scale=dec[gs][0:64, t:t + 1])
                            return e.activation(out=tmpS[:, 2 + sl, :], in_=p_st[:, 1, :], func=AF.Identity,
                                                scale=dec[gs][:, 8 + t:9 + t])
                        P.op("act", tmp, reads=[res("p_st"), Rdec], writes=[Rtmp])

                        def om(e):
                            e.matmul(p_ot[0][:, c0:c1], lhsT=v_sb[gs][:, t, 0:128],
                                     rhs=scm[ks][:, 0:128], start=True, stop=False)
                            e.matmul(p_ot[0][:, c0:c1], lhsT=Sb[si][0:64, 0:128], rhs=q_a[0:64, c0:c1],
                                     start=False, stop=True)
                            e.matmul(p_ot[1][:, c0:c1], lhsT=v_sb[gs][:, t, 128:256],
                                     rhs=scm[ks][:, 128:256], start=True, stop=False)
                            return e.matmul(p_ot[1][:, c0:c1], lhsT=Sb[si][:, 128:256], rhs=q_b[:, c0:c1],
                                            start=False, stop=True)
                        P.op("pe", om, reads=[Rv, Rscm, Rqk, res("Sb%d" % si)],
                             writes=[res("p_ot0"), res("p_ot1")])

                        def upd(e):
                            e.scalar_tensor_tensor(out=S32[0:64, 0:128], in0=S32[0:64, 0:128],
                                                   scalar=dec[gs][0:64, t:t + 1], in1=tmpS[0:64, sl, :],
                                                   op0=ALU.mult, op1=ALU.add)
                            return e.scalar_tensor_tensor(out=S32[:, 128:256], in0=S32[:, 128:256],
                                                          scalar=dec[gs][:, 8 + t:9 + t], in1=tmpS[:, 2 + sl, :],
                                                          op0=ALU.mult, op1=ALU.add)
                        P.op("dve", upd, reads=[Rtmp, Rdec, res("S32")], writes=[res("S32")])
                        P.op("dve", lambda e: e.tensor_copy(out=Sb[so][:], in_=S32[:]),
                             reads=[res("S32")], writes=[res("Sb%d" % so)])
                    tile_pcs.append(chunk)
                    pcs.append(tile_pcs)

                tiles = pcs
                pcs = [tiles[0][0]]
                for t in range(4):
                    if t + 1 < 4:
                        pcs.append(tiles[t + 1][0])
                    pcs += tiles[t][1:]
                ys = G % 2
                q, gq = G // 4, G % 4
                Ry = res("yT0")
                n1b = []
                for h in range(2):
                    def n1a(h=h):
                        P.op("act", lambda e: e.copy(out=tD[h][:], in_=p_ot[h][:]),
                             reads=[res("p_ot%d" % h)], writes=[res("tD%d" % h)])

                    def n1(h=h):
                        Rt = res("tD%d" % h)
                        P.op("act", lambda e: e.activation(out=osq[:], in_=tD[h][:], func=AF.Square),
                             reads=[Rt], writes=[res("osq")])
                        bank, Rb = fm_bank()
                        P.op("pe", lambda e: e.matmul(
                            bank[:], lhsT=ones_bf[:], rhs=osq[:], start=True, stop=True),
                            reads=[res("osq"), R_const], writes=[Rb])
                        P.op("act", lambda e: e.activation(
                            out=tD[2][:], in_=bank[:], func=AF.Ln, scale=1.0 / 128, bias=eps_t[:, 0:1]),
                            reads=[Rb, R_const], writes=[res("tD2")])
                        P.op("act", lambda e: e.activation(out=tD[2][:], in_=tD[2][:], func=AF.Exp, scale=-0.5),
                             reads=[res("tD2")], writes=[res("tD2")])
                        P.op("dve", lambda e: e.tensor_mul(out=tD[2][:], in0=tD[h][:], in1=tD[2][:]),
                             reads=[Rt, res("tD2")], writes=[res("tD2")])
                        P.op("dve", lambda e: e.scalar_tensor_tensor(
                            out=yT[0][:, h, :], in0=tD[2][:], scalar=gnw_sb[:, h:h + 1], in1=sg[ys][h][:],
                            op0=ALU.mult, op1=ALU.mult),
                            reads=[res("tD2"), res("sg%d_%d" % (ys, h)), res("small")], writes=[Ry])
                    pcs.append(n1a)
                    n1b.append(n1)

                def store():
                    P.dma("sp", lambda e, then: then(e.dma_start(
                        out=ybounce[q, :, gq * GT:(gq + 1) * GT].rearrange("(h p) n -> p h n", p=128),
                        in_=yT[0][:])), Ry, reads=[Ry], writes=[R_yb[q]])
                    if gq == 3:
                        P.collective(lambda e: e.collective_compute(
                            "AllGather", ALU.bypass, replica_groups=[[0, 1, 2, 3], [4, 5, 6, 7]],
                            ins=[ybounce[q]], outs=[agout[q]]),
                            reads=[R_yb[q]], writes=[R_ag[q]])
                return pcs, n1b + [store]

            def interleave(main, bulk, deferred=None):
                nm, nb = len(main), len(bulk)
                j = 0
                for i, m in enumerate(main):
                    m()
                    if deferred and i in DEFER_AT:
                        deferred[DEFER_AT.index(i)]()
                    tgt = ((i + 1) * nb) // nm
                    while j < tgt:
                        bulk[j]()
                        j += 1
                while j < nb:
                    bulk[j]()
                    j += 1

            def merge(b, c):
                outl = []
                nb, ncc = len(b), len(c)
                j = 0
                for i, f in enumerate(b):
                    outl.append(f)
                    tgt = ((i + 1) * ncc) // max(nb, 1)
                    while j < tgt:
                        outl.append(c[j])
                        j += 1
                outl += c[j:]
                return outl

            EARLY_X = True
            for n in range(4 if EARLY_X else NXP):
                x_load(n)
            for f in x_pieces(0):
                f()
            if EARLY_X:
                for n in range(4, 8):
                    x_load(n)
            for f in merge(pipeline_b(e_pieces(0)), x_pieces(1)):
                f()
            pid_m = [None]

            def prefetch_y0():
                def ld_y0(e, then):
                    if pid_m[0] is None:
                        pid_m[0] = e.partition_id()
                    src = agout[0].rearrange("f (s n) -> s f n", n=GT)[bass.ds(pid_m[0] % 4, 1), :, :]
                    then(e.dma_start(out=yin0[:], in_=src.rearrange("a (j p) n -> p (a j) n", p=128)))
                P.dma("pool", ld_y0, R_yin0, reads=[R_ag[0]], writes=[R_yin0])

            pending = None
            for G in range(NG_MIX):
                big_loads(G)
                if G == 9:
                    prefetch_y0()
                bulk_b = pipeline_b(e_pieces(G + 1)) if G + 1 < NG_MIX else []
                bulk_c = x_pieces(G + 2) if G + 2 < NG_MIX else []
                if EARLY_X and G + 2 < NG_MIX:
                    for n in range(4 * (G + 2), 4 * (G + 2) + 4):
                        x_load(n)
                apcs, tail = a_pieces(G)
                interleave(apcs, merge(bulk_c, bulk_b[:len(bulk_b) // 2]) + bulk_b[len(bulk_b) // 2:], pending)
                pending = tail
            for f in pending:
                f()

            P.finalize()
            with nc.Block() as block:
                @block.sync
                def _(e):
                    P.run("sp", e)

                @block.scalar
                def _(e):
                    P.run("act", e)

                @block.vector
                def _(e):
                    P.run("dve", e)

                @block.gpsimd
                def _(e):
                    P.run("pool", e)

                @block.tensor
                def _(e):
                    P.run("pe", e)

        for r_ in [R_wout, R_wdown, R_ident, R_const, R_wupb, R_yin0] + R_yb + R_ag:
            r_.ws = [ev for ev in r_.ws if ev.dma]
            r_.rs = [ev for ev in r_.rs if ev.dma]
        with contextlib.ExitStack() as es:
            nbc = [sb(es, "nbc%d" % i, [128, D]) for i in range(3)]
            wupc = [sb(es, "wupc%d" % i, [128, 8, 512], BF16) for i in range(2)]
            yin = [yin0, sb(es, "yin1", [128, 8, GT], BF16)]
            NH = 6
            h1 = [sb(es, "h1_%d" % i, [128, D]) for i in range(NH)]
            fst = [sb(es, "fst%d" % i, [128, 8]) for i in range(NH)]
            junk2 = sb(es, "junk2", [128, D], BF16)
            hn1b = [sb(es, "hn1b%d" % i, [128, D], BF16) for i in range(2)]
            hn1T = sb(es, "hn1T", [128, 8, GT], BF16)
            aT = sb(es, "aT", [128, 32, GT], BF16)
            rl = [sb(es, "rl%d" % i, [128, GT]) for i in range(2)]
            mtmp = [sb(es, "mtmp%d" % i, [128, D]) for i in range(2)]

            q_tr = ps(es, "q_tr", [128, 8, 128], BF16)
            q_mix = ps(es, "q_mix", [128, D])
            q_up = [ps(es, "q_up%d" % i, [128, GT]) for i in range(3)]
            q_dn = ps(es, "q_dn", [128, D])

            R = {"yin0": R_yin0}

            def res(name):
                if name not in R:
                    R[name] = Res(name)
                return R[name]

            def ld_nbc(i):
                P.dma("sp", lambda e, then: then(e.dma_start(
                    out=nbc[i][:], in_=nrm[i + 1:i + 2, :].partition_broadcast(128))),
                    res("nbc%d" % i), writes=[res("nbc%d" % i)])

            pid = [None]
            wu_n = [0]

            def load_wup(fc4):
                s = wu_n[0] % 2
                wu_n[0] += 1
                Rw = res("wupc%d" % s)
                P.dma("sp", lambda e, then, s=s, fc4=fc4: then(e.dma_start(
                    out=wupc[s][:], in_=wupb[:, fc4 * 512:(fc4 + 1) * 512].rearrange("(c p) n -> p c n", p=128))),
                    Rw, reads=[R_wupb], writes=[Rw])
                return s, Rw

            def rms_from(src_ap, src_res, st, col, junk_res):
                Rst = res("fst%d" % st)
                P.op("act", lambda e: e.activation(out=junk2[:], in_=src_ap, func=AF.Square,
                                                   accum_out=fst[st][:, col:col + 1]),
                     reads=[src_res], writes=[junk_res, Rst])
                P.op("act", lambda e: e.activation(out=fst[st][:, col + 1:col + 2], in_=fst[st][:, col:col + 1],
                                                   func=AF.Ln, scale=1.0 / D, bias=eps_t[:, 0:1]),
                     reads=[Rst, R_const], writes=[Rst])
                P.op("act", lambda e: e.activation(out=fst[st][:, col + 2:col + 3], in_=fst[st][:, col + 1:col + 2],
                                                   func=AF.Exp, scale=-0.5),
                     reads=[Rst], writes=[Rst])
                return Rst

            out_evs = []

            def load_y(g):
                ysl = g % 2
                Ryin = res("yin%d" % ysl)

                def ld_y(e, then, g=g, ysl=ysl):
                    if pid[0] is None:
                        pid[0] = e.partition_id()
                    src = agout[g].rearrange("f (s n) -> s f n", n=GT)[bass.ds(pid[0] % 4, 1), :, :]
                    then(e.dma_start(out=yin[ysl][:], in_=src.rearrange("a (j p) n -> p (a j) n", p=128)))
                P.dma("pool", ld_y, Ryin, reads=[R_ag[g]], writes=[Ryin])

            def front_x(g, t):
                n = g * 4 + t
                hs = n % NH
                Rh1 = res("h1_%d" % hs)
                row0 = g * GT + t * 128
                P.dma("sp", lambda e, then: then(
                    e.dma_start(out=h1[hs][:], in_=xo[row0:row0 + 128, :])), Rh1, writes=[Rh1])

            def front_a(g, t, load=True):
                ysl = g % 2
                Ryin = res("yin%d" % ysl)
                n = g * 4 + t
                hs = n % NH
                Rh1 = res("h1_%d" % hs)
                if load:
                    front_x(g, t)

                def mmix(e):
                    ins = None
                    for half in range(2):
                        for j in range(8):
                            c = (j % 2) * 4 + (j // 2)
                            ins = e.matmul(q_mix[:, half * 512:(half + 1) * 512],
                                           lhsT=yin[ysl][:, j, t * 128:(t + 1) * 128],
                                           rhs=w_out_sb[:, c, half * 512:(half + 1) * 512],
                                           start=(j == 0), stop=(j == 7))
                    return ins
                P.op("pe", mmix, reads=[Ryin, R_wout], writes=[res("q_mix")])
                Rst = rms_from(q_mix[:], res("q_mix"), hs, 0, res("junk2"))
                ms = n % 2
                Rm = res("mtmp%d" % ms)
                P.op("dve", lambda e: e.scalar_tensor_tensor(
                    out=mtmp[ms][:], in0=q_mix[:], scalar=fst[hs][:, 2:3], in1=nbc[0][:],
                    op0=ALU.mult, op1=ALU.mult),
                    reads=[res("q_mix"), Rst, res("nbc0")], writes=[Rm])
                P.op("dve", lambda e: e.tensor_add(out=h1[hs][:], in0=h1[hs][:], in1=mtmp[ms][:]),
                     reads=[Rm, Rh1], writes=[Rh1])
                Rst = rms_from(h1[hs][:], Rh1, hs, 3, res("junk2"))
                hb = n % 2
                Rhb = res("hn1b%d" % hb)
                P.op("dve", lambda e: e.scalar_tensor_tensor(
                    out=hn1b[hb][:], in0=h1[hs][:], scalar=fst[hs][:, 5:6], in1=nbc[1][:],
                    op0=ALU.mult, op1=ALU.mult),
                    reads=[Rh1, Rst, res("nbc1")], writes=[Rhb])


            def front_b(g, t):
                n = g * 4 + t
                hb = n % 2
                Rhb = res("hn1b%d" % hb)

                def tr(e):
                    ins = None
                    for c in range(8):
                        ins = e.transpose(out=q_tr[:, c, :], in_=hn1b[hb][:, c * 128:(c + 1) * 128],
                                          identity=ident[:])
                    return ins
                P.op("pe", tr, reads=[Rhb, R_ident], writes=[res("q_tr")])
                P.op("act", lambda e: e.copy(out=hn1T[:, :, t * 128:(t + 1) * 128], in_=q_tr[:]),
                     reads=[res("q_tr")], writes=[res("hn1T")])

            wup_pre = {}

            def prefetch_wup(g):
                wup_pre[g] = [load_wup(0), load_wup(1)]

            def up(g, pipe):
                for fc4 in range(8):
                    if pipe and fc4 == 7 and g + 1 < NG_FFN:
                        front_a(g + 1, 0)
                    if g in wup_pre and fc4 < 2:
                        s, Rw = wup_pre[g][fc4]
                    else:
                        s, Rw = load_wup(fc4)
                    for i in range(4):
                        fc = fc4 * 4 + i
                        ub = fc % 3
                        Rub = res("q_up%d" % ub)

                        def mup(e, s=s, i=i, ub=ub):
                            ins = None
                            for k in range(8):
                                ins = e.matmul(q_up[ub][:], lhsT=wupc[s][:, k, i * 128:(i + 1) * 128],
                                               rhs=hn1T[:, k, :], start=(k == 0), stop=(k == 7))
                            return ins
                        P.op("pe", mup, reads=[Rw, res("hn1T")], writes=[Rub])
                        rs = fc % 2
                        Rrl = res("rl%d" % rs)
                        P.op("act", lambda e, ub=ub, rs=rs: e.activation(out=rl[rs][:], in_=q_up[ub][:], func=AF.Relu),
                             reads=[Rub], writes=[Rrl])
                        P.op("dve", lambda e, rs=rs, fc=fc: e.tensor_mul(out=aT[:, fc, :], in0=rl[rs][:], in1=rl[rs][:]),
                             reads=[Rrl], writes=[res("aT")])

            def down(g, t):
                n = g * 4 + t
                hs = n % NH
                Rh1 = res("h1_%d" % hs)

                def mdn(e):
                    ins = None
                    for half in range(2):
                        for fc in range(32):
                            ins = e.matmul(q_dn[:, half * 512:(half + 1) * 512],
                                           lhsT=aT[:, fc, t * 128:(t + 1) * 128],
                                           rhs=w_down_sb[:, fc, half * 512:(half + 1) * 512],
                                           start=(fc == 0), stop=(fc == 31))
                    return ins
                P.op("pe", mdn, reads=[res("aT"), R_wdown], writes=[res("q_dn")])
                Rst = rms_from(q_dn[:], res("q_dn"), hs, 0, res("junk2"))
                ms = n % 2
                Rm = res("mtmp%d" % ms)
                P.op("dve", lambda e: e.scalar_tensor_tensor(
                    out=mtmp[ms][:], in0=q_dn[:], scalar=fst[hs][:, 2:3], in1=nbc[2][:],
                    op0=ALU.mult, op1=ALU.mult),
                    reads=[res("q_dn"), Rst, res("nbc2")], writes=[Rm])
                P.op("dve", lambda e: e.tensor_add(out=h1[hs][:], in0=h1[hs][:], in1=mtmp[ms][:]),
                     reads=[Rm, Rh1], writes=[Rh1])
                row0 = g * GT + t * 128
                ev = P.dma("sp", lambda e, then: then(
                    e.dma_start(out=out[row0:row0 + 128, :], in_=h1[hs][:])), Rh1, reads=[Rh1])
                out_evs.append(ev)

            FFN_PIPE = True
            front_x(0, 0)
            ld_nbc(0)
            ld_nbc(1)
            ld_nbc(2)
            front_a(0, 0, load=False)
            for t in range(4):
                if t + 1 < 4:
                    front_a(0, t + 1)
                if t == 0:
                    prefetch_wup(0)
                    load_y(1)
                front_b(0, t)
            for g in range(NG_FFN):
                up(g, FFN_PIPE)
                nxt = g + 1 < NG_FFN
                if nxt:
                    prefetch_wup(g + 1)
                if g + 2 < NG_FFN:
                    load_y(g + 2)
                for t in range(4):
                    down(g, t)
                    if nxt and FFN_PIPE:
                        front_b(g + 1, t)
                        if t + 1 < 4:
                            front_a(g + 1, t + 1)
                    elif nxt:
                        front_a(g + 1, t)
                        front_b(g + 1, t)

            P.wait_all("sp", out_evs)
            P.finalize()
            with nc.Block() as block:
                @block.sync
                def _(e):
                    P.run("sp", e)

                @block.scalar
                def _(e):
                    P.run("act", e)

                @block.vector
                def _(e):
                    P.run("dve", e)

                @block.gpsimd
                def _(e):
                    P.run("pool", e)

                @block.tensor
                def _(e):
                    P.run("pe", e)
    return nc


_NC_CACHE = {}


def kernel(x, w_in, w_gk_up, b_gk, gla_norm_w, hgrn_norm_w, hgrn_lower_bounds, w_out,
           pre_mix_norm, post_mix_norm, pre_mlp_norm, post_mlp_norm, w_up, w_down):
    f = lambda a: np.ascontiguousarray(np.asarray(a, dtype=np.float32))
    x = f(x)
    w_in0 = f(w_in)[0]
    o_gq, o_gk, o_gv, o_gg, o_glr, o_hq, o_hf, o_hi, o_hg = 0, 256, 512, 1024, 1536, 1552, 2064, 2576, 3088
    nrm = f(np.stack([np.asarray(pre_mix_norm)[0], np.asarray(post_mix_norm)[0],
                      np.asarray(pre_mlp_norm)[0], np.asarray(post_mlp_norm)[0]], 0))
    jj, ii = np.meshgrid(np.arange(128), np.arange(128), indexing="ij")
    cm = ((jj <= ii) & (jj // CH == ii // CH)).astype(np.float32)
    cmask = f(np.concatenate([cm, cm], 1))
    ident = np.eye(128, dtype=np.float32)
    w_out0, w_up0, w_down0 = f(w_out)[0], f(w_up)[0], f(w_down)[0]
    in_maps = []
    for c in range(8):
        b, hg = c // 4, c % 4
        a64 = slice(hg * 64, hg * 64 + 64)
        a128 = slice(hg * 128, hg * 128 + 128)
        cols = lambda o, s: w_in0[:, o + s.start:o + s.stop]
        wfm = np.concatenate([cols(o_gq, a64), cols(o_gk, a64), cols(o_gg, a128), cols(o_hq, a128),
                              cols(o_hf, a128), cols(o_hg, a128), w_in0[:, o_glr:o_glr + 16]], 1)
        wtm = np.concatenate([cols(o_gv, a128), cols(o_hi, a128)], 1)
        wgk = np.concatenate([f(w_gk_up)[0][:, a64], f(b_gk)[0][None, a64]], 0)
        lbp = f(hgrn_lower_bounds)[:, a128].T
        gnw = np.stack([f(gla_norm_w)[0][a128], f(hgrn_norm_w)[0][a128]], 1)
        in_maps.append({
            "xb": x[b],
            "xo": f(np.concatenate([x[b, q * TOK + hg * GT:q * TOK + (hg + 1) * GT] for q in range(4)], 0)),
            "wfm": f(wfm), "wtm": f(wtm), "wgk": f(wgk), "lbp": f(lbp), "gnw": f(gnw), "nrm": nrm,
            "w_out": w_out0, "w_up": w_up0, "w_down": w_down0, "cmask": cmask, "ident": ident,
        })
    if "nc" not in _NC_CACHE:
        _NC_CACHE["nc"] = build_nc()
    res = run_bass_kernel_spmd(_NC_CACHE["nc"], in_maps, core_ids=list(range(8)))
    outp = np.empty((2, SEQ, D), np.float32)
    for c in range(8):
        b, s = c // 4, c % 4
        o_c = res.results[c]["out"]
        for q in range(4):
            outp[b, q * TOK + s * GT:q * TOK + (s + 1) * GT] = o_c[q * GT:(q + 1) * GT]
    return outp
```

```python
import contextlib
import numpy as np
import concourse.bass as bass
import concourse.mybir as mybir
from concourse.bass_utils import run_bass_kernel_spmd

F32 = mybir.dt.float32
BF16 = mybir.dt.bfloat16
ALU = mybir.AluOpType
AF = mybir.ActivationFunctionType

D = 1024
SEQ = 8192
TOK = 2048
GT = 512
NG_MIX = SEQ // GT
NG_FFN = TOK // GT
DFF = 4096
EPS = 1e-6
CH = 128
NFM = 656


class Ev:
    __slots__ = ("sem", "val", "seq", "eng", "used", "dma")

    def __init__(self, sem, seq, eng, dma=False):
        self.sem, self.seq, self.eng, self.dma = sem, seq, eng, dma
        self.val = None
        self.used = dma


class Res:
    __slots__ = ("name", "ws", "rs", "dsem", "dseq")

    def __init__(self, name):
        self.name = name
        self.ws, self.rs = [], []
        self.dsem, self.dseq = None, 0


class Prog:
    ENGS = ("pe", "act", "dve", "pool", "sp")

    def __init__(self, nc):
        self.nc = nc
        self.q = {e: [] for e in self.ENGS}
        self.sem = {e: nc.alloc_semaphore("prog_" + e) for e in self.ENGS}
        self.seq = {e: 0 for e in self.ENGS}
        self.ccsem = nc.alloc_semaphore("ccsem")
        self.ccseq = 0
        self.cnt = {e: 0 for e in self.ENGS}

    def _deps(self, eng, reads, writes):
        waits = []
        for r in reads:
            for ev in r.ws:
                if ev.eng == eng and eng == "pe":
                    continue
                waits.append(ev)
        for w in writes:
            for ev in w.rs + w.ws:
                if ev.eng == eng:
                    continue
                waits.append(ev)
        return waits

    @staticmethod
    def _commit(ev, reads, writes):
        for r in reads:
            r.rs.append(ev)
        for w in writes:
            if w.rs:
                w.ws, w.rs = [ev], []
            else:
                w.ws.append(ev)
                if len(w.ws) > 24:
                    w.ws = w.ws[-24:]

    def op(self, eng, fn, reads=(), writes=()):
        waits = self._deps(eng, reads, writes)
        self.seq[eng] += 1
        ev = Ev(self.sem[eng], self.seq[eng], eng)
        self.q[eng].append([waits, fn, ev, 1])
        self._commit(ev, reads, writes)
        return ev

    def dma(self, eng, fn, primary, reads=(), writes=(), n=1):
        waits = self._deps(eng, reads, writes)
        if primary.dsem is None:
            primary.dsem = self.nc.alloc_semaphore("d_" + primary.name)
        primary.dseq += n
        ev = Ev(primary.dsem, primary.dseq, None, dma=True)
        ev.val = 16 * primary.dseq
        self.q[eng].append([waits, fn, ev, 16])
        self._commit(ev, reads, writes)
        return ev

    def collective(self, fn, reads=(), writes=()):
        waits = self._deps("pool", reads, writes)
        self.ccseq += 1
        ev = Ev(self.ccsem, self.ccseq, None, dma=True)
        ev.val = self.ccseq
        self.q["pool"].append([waits, fn, ev, 0])
        self._commit(ev, reads, writes)
        return ev

    def wait_all(self, eng, evs):
        self.q[eng].append([list(evs), None, None, 0])

    def finalize(self):
        for eng in self.ENGS:
            seen = {}
            for item in self.q[eng]:
                kept = []
                for ev in item[0]:
                    k = id(ev.sem)
                    if seen.get(k, 0) >= ev.seq:
                        continue
                    seen[k] = ev.seq
                    ev.used = True
                    kept.append(ev)
                item[0] = kept
        for eng in self.ENGS:
            for item in self.q[eng]:
                ev = item[2]
                if ev is not None and not ev.dma and ev.used:
                    self.cnt[eng] += 1
                    ev.val = self.cnt[eng]

    def run(self, eng, e):
        for waits, fn, ev, kind in self.q[eng]:
            for w in waits:
                e.wait_ge(w.sem, w.val)
            if fn is None:
                continue
            if kind == 16:
                fn(e, lambda ins, s=ev.sem: ins.then_inc(s, 16))
            elif kind == 0:
                fn(e).then_inc(ev.sem)
            else:
                ins = fn(e)
                if ev.used:
                    ins.then_inc(ev.sem, 1)
        self.q[eng] = []


def build_nc():
    nc = bass.Bass("TRN2", target_bir_lowering=False)

    def din(name, shape, dt=F32):
        return nc.dram_tensor(name, shape, dt, kind="ExternalInput").ap()

    xb = din("xb", [SEQ, D])
    xo = din("xo", [TOK, D])
    wfm = din("wfm", [D, NFM])
    wtm = din("wtm", [D, 256])
    wgk = din("wgk", [17, 64])
    lbp = din("lbp", [128, 2])
    gnw = din("gnw", [128, 2])
    nrm = din("nrm", [4, D])
    w_out = din("w_out", [D, D])
    w_up = din("w_up", [D, DFF])
    w_down = din("w_down", [DFF, D])
    cmask_in = din("cmask", [128, 256])
    ident_in = din("ident", [128, 128])
    out = nc.dram_tensor("out", [TOK, D], F32, kind="ExternalOutput").ap()

    ybounce = nc.dram_tensor("ybounce", [4, 256, TOK], BF16).ap()
    agout = nc.dram_tensor("agout", [4, 1024, TOK], BF16).ap()
    wupb = nc.dram_tensor("wupb", [D, DFF], BF16).ap()

    P = Prog(nc)
    es_outer = contextlib.ExitStack()

    def sb(es, name, shape, dt=F32):
        return es.enter_context(nc.sbuf_tensor(name, shape, dt))

    def ps(es, name, shape, dt=F32):
        return es.enter_context(nc.psum_tensor(name, shape, dt))

    with es_outer:
        w_out_sb = sb(es_outer, "w_out_sb", [128, 8, D], BF16)
        w_down_sb = sb(es_outer, "w_down_sb", [128, 32, D], BF16)
        ident = sb(es_outer, "ident_sb", [128, 128], BF16)
        eps_t = sb(es_outer, "eps_t", [128, 1])
        one_t = sb(es_outer, "one_t", [128, 1])
        R_wout, R_wdown, R_ident, R_const = Res("wout"), Res("wdown"), Res("ident"), Res("const")
        R_wupb = Res("wupb")
        yin0 = sb(es_outer, "yin0", [128, 8, GT], BF16)
        R_yin0 = Res("yin0")
        R_yb = [Res("yb%d" % q) for q in range(4)]
        R_ag = [Res("ag%d" % q) for q in range(4)]

        with contextlib.ExitStack() as es:
            wfm_sb = sb(es, "wfm_sb", [128, 8, NFM], BF16)
            wtm_sb = sb(es, "wtm_sb", [128, 8, 256], BF16)
            wgk_sb = sb(es, "wgk_sb", [17, 64])
            lbt = sb(es, "lbt", [128, 2])
            lbw = sb(es, "lbw", [128, 8])
            gnw_sb = sb(es, "gnw_sb", [128, 2])
            wbc0 = sb(es, "wbc0", [128, D])
            scanmask = sb(es, "scanmask", [128, GT])
            cmask = sb(es, "cmask_sb", [128, 256])
            ones_bf = sb(es, "ones_bf", [128, 128], BF16)
            NX = 4
            xt = [sb(es, "xt%d" % i, [128, D]) for i in range(NX)]
            stat = [sb(es, "stat%d" % i, [128, 4]) for i in range(NX)]
            hnb = [sb(es, "hnb%d" % i, [128, D], BF16) for i in range(2)]
            hnT = [sb(es, "hnT%d" % i, [128, 8, GT], BF16) for i in range(2)]
            glrT = sb(es, "glrT", [17, GT])
            tA = [sb(es, "tA%d" % i, [128, GT]) for i in range(4)]
            tB = [sb(es, "tB%d" % i, [128, GT]) for i in range(6)]
            tC = [sb(es, "tC%d" % i, [128, GT]) for i in range(1)]
            sg = [[sb(es, "sg%d_%d" % (i, j), [128, GT]) for j in range(2)] for i in range(2)]
            tD = [sb(es, "tD%d" % i, [128, GT]) for i in range(3)]
            osq = sb(es, "osq", [128, GT], BF16)
            dec = [sb(es, "dec%d" % i, [128, 16]) for i in range(2)]
            qk = [[sb(es, "qk%d_%d" % (i, j), [128, GT], BF16) for j in range(4)] for i in range(2)]
            v_sb = [sb(es, "v_sb%d" % i, [128, 4, 256], BF16) for i in range(2)]
            kd_tok = [sb(es, "kd_tok%d" % i, [128, 192], BF16) for i in range(2)]
            scm = [sb(es, "scm%d" % i, [128, 256], BF16) for i in range(2)]
            S32 = sb(es, "S32", [128, 256])
            Sb = [sb(es, "Sb%d" % i, [128, 256], BF16) for i in range(4)]
            tmpS = sb(es, "tmpS", [128, 4, 128])
            yT = [sb(es, "yT%d" % i, [128, 2, GT], BF16) for i in range(1)]

            p_tr = ps(es, "p_tr", [128, 8, 128], BF16)
            p_fm = [ps(es, "p_fm%d" % i, [128, GT]) for i in range(2)]
            p_ks = ps(es, "p_ks", [128, GT])
            p_kt = p_ks[:, 256:512].bitcast(BF16)
            p_vg = ps(es, "p_vg", [128, GT])
            p_ot = [ps(es, "p_ot%d" % i, [128, GT]) for i in range(2)]
            p_st = ps(es, "p_st", [128, 4, 128])

            R = {}

            def res(name):
                if name not in R:
                    R[name] = Res(name)
                return R[name]

            P.dma("sp", lambda e, then: then(e.dma_start(out=xt[0][:], in_=xb[0:128, :])),
                  res("xt0"), writes=[res("xt0")])
            P.dma("sp", lambda e, then: then(e.dma_start(
                out=wbc0[:], in_=nrm[0:1, :].partition_broadcast(128))), res("wbc0"), writes=[res("wbc0")])

            def ld_small(e, then):
                then(e.dma_start(out=wgk_sb[:], in_=wgk))
                then(e.dma_start(out=lbt[:], in_=lbp))
                then(e.dma_start(out=gnw_sb[:], in_=gnw))
                then(e.dma_start(out=cmask[:], in_=cmask_in))
            P.dma("sp", ld_small, res("small"), writes=[res("small")], n=4)
            P.dma("pool", lambda e, then: then(e.dma_start(out=ident[:], in_=ident_in)),
                  R_ident, writes=[R_ident])
            P.dma("pool", lambda e, then: then(e.dma_start(
                out=wfm_sb[:], in_=wfm.rearrange("(c p) n -> p c n", p=128))),
                res("wfm"), writes=[res("wfm")])
            P.dma("pool", lambda e, then: then(e.dma_start(
                out=wtm_sb[:], in_=wtm.rearrange("(c p) n -> p c n", p=128))),
                res("wtm"), writes=[res("wtm")])

            def consts(e):
                e.memset(eps_t[:], EPS)
                e.memset(one_t[:], 1.0)
                e.memset(scanmask[:].rearrange("p (c k) -> p c k", k=CH)[:, :, 1:CH], 1.0)
                e.memset(scanmask[:].rearrange("p (c k) -> p c k", k=CH)[:, :, 0:1], 0.0)
                e.memset(ones_bf[:], 1.0)
                e.memset(glrT[:], 1.0)
                e.memset(S32[:], 0.0)
                e.memset(Sb[0][:], 0.0)
                e.memset(Sb[1][:], 0.0)
                e.memset(Sb[2][:], 0.0)
                return e.memset(Sb[3][:], 0.0)
            P.op("dve", consts, writes=[R_const, res("glrT"), res("S32"), res("Sb0"), res("Sb1"),
                                        res("Sb2"), res("Sb3")])

            P.op("dve", lambda e: e.tensor_sub(out=lbw[:, 0:1], in0=lbt[:, 1:2], in1=lbt[:, 0:1]),
                 reads=[res("small")], writes=[res("lbw0")])
            P.op("act", lambda e: e.activation(out=lbw[:, 1:2], in_=lbw[:, 0:1], func=AF.Exp),
                 reads=[res("lbw0")], writes=[res("lbw1")])
            P.op("dve", lambda e: e.tensor_scalar_add(out=lbw[:, 2:3], in0=lbw[:, 1:2], scalar1=1.0),
                 reads=[res("lbw1")], writes=[res("lbw2")])
            P.op("dve", lambda e: e.reciprocal(out=lbw[:, 2:3], in_=lbw[:, 2:3]),
                 reads=[res("lbw2")], writes=[res("lbw2")])
            P.op("dve", lambda e: e.tensor_mul(out=lbw[:, 3:4], in0=lbw[:, 1:2], in1=lbw[:, 2:3]),
                 reads=[res("lbw1"), res("lbw2")], writes=[res("omlb")])
            omlb = lbw[:, 3:4]
            P.op("dve", lambda e: e.tensor_scalar_mul(out=lbw[:, 4:5], in0=lbw[:, 3:4], scalar1=-1.0),
                 reads=[res("omlb")], writes=[res("omlb")])
            nomlb = lbw[:, 4:5]

            def big_loads(G):
                if G == 0:
                    P.dma("pool", lambda e, then: then(e.dma_start(
                        out=w_out_sb[:], in_=w_out.rearrange("(c p) n -> p c n", p=128))),
                        R_wout, writes=[R_wout])
                elif 1 <= G <= 4:
                    i = G - 1
                    P.dma("pool", lambda e, then, i=i: then(e.dma_start(
                        out=w_down_sb[:, 8 * i:8 * i + 8, :],
                        in_=w_down[1024 * i:1024 * (i + 1), :].rearrange("(c p) n -> p c n", p=128))),
                        R_wdown, writes=[R_wdown])
                elif 5 <= G <= 12:
                    i = G - 5
                    P.dma("pool", lambda e, then, i=i: then(e.dma_start(
                        out=wupb[128 * i:128 * (i + 1), :], in_=w_up[128 * i:128 * (i + 1), :])),
                        R_wupb, writes=[R_wupb])

            fm_rr = [0]

            fm_pend = [False, False]

            def fm_bank(hold=False):
                i = fm_rr[0] % 2
                if fm_pend[i]:
                    i = 1 - i
                assert not fm_pend[i]
                fm_rr[0] = i + 1
                fm_pend[i] = hold
                return p_fm[i], res("p_fm%d" % i)

            def fm_release(bank):
                for i in range(2):
                    if bank is p_fm[i]:
                        fm_pend[i] = False

            NXP = 3
            PE_AHEAD = True
            DEFER_AT = [0, 2, 5]

            def x_load(n):
                if n >= NG_MIX * 4:
                    return
                xs = n % NX
                tok0 = n * 128
                Rx = res("xt%d" % xs)
                P.dma("sp", lambda e, then: then(
                    e.dma_start(out=xt[xs][:], in_=xb[tok0:tok0 + 128, :])), Rx, writes=[Rx])

            def x_pieces(G):
                h3 = G % 2
                pcs = []
                for t in range(4):
                    n = G * 4 + t
                    xs = n % NX
                    hb = n % 2
                    Rx, Rst, Rhb = res("xt%d" % xs), res("stat%d" % xs), res("hnb%d" % hb)

                    def ca(n=n, xs=xs, hb=hb, Rx=Rx, Rst=Rst, Rhb=Rhb):
                        P.op("act", lambda e: e.activation(
                            out=hnb[hb][:], in_=xt[xs][:], func=AF.Square, accum_out=stat[xs][:, 0:1]),
                            reads=[Rx], writes=[Rhb, Rst])
                        P.op("act", lambda e: e.activation(
                            out=stat[xs][:, 1:2], in_=stat[xs][:, 0:1], func=AF.Ln,
                            scale=1.0 / D, bias=eps_t[:, 0:1]),
                            reads=[Rst, R_const], writes=[Rst])
                        P.op("act", lambda e: e.activation(
                            out=stat[xs][:, 2:3], in_=stat[xs][:, 1:2], func=AF.Exp, scale=-0.5),
                            reads=[Rst], writes=[Rst])
                        P.op("dve", lambda e: e.scalar_tensor_tensor(
                            out=hnb[hb][:], in0=xt[xs][:], scalar=stat[xs][:, 2:3], in1=wbc0[:],
                            op0=ALU.mult, op1=ALU.mult),
                            reads=[Rx, Rst, res("wbc0")], writes=[Rhb])
                        if not EARLY_X:
                            x_load(n + NXP)

                    def cb(t=t, hb=hb, Rhb=Rhb):
                        def tr(e):
                            ins = None
                            for c in range(8):
                                ins = e.transpose(out=p_tr[:, c, :], in_=hnb[hb][:, c * 128:(c + 1) * 128],
                                                  identity=ident[:])
                            return ins
                        P.op("pe", tr, reads=[Rhb, R_ident], writes=[res("p_tr")])
                        P.op("dve", lambda e: e.tensor_copy(
                            out=hnT[h3][:, :, t * 128:(t + 1) * 128], in_=p_tr[:]),
                            reads=[res("p_tr")], writes=[res("hnT%d" % h3)])
                    pcs += [ca, cb]
                return pcs

            def proj_fm(G, col0, M):
                h3 = G % 2
                bank, Rb = fm_bank(hold=True)

                def mm(e):
                    ins = None
                    for c in range(8):
                        ins = e.matmul(bank[0:M, :], lhsT=wfm_sb[:, c, col0:col0 + M],
                                       rhs=hnT[h3][:, c, :], start=(c == 0), stop=(c == 7))
                    return ins
                P.op("pe", mm, reads=[res("hnT%d" % h3), res("wfm")], writes=[Rb])
                return bank, Rb

            def e_pieces(G):
                gs = G % 2
                h3 = G % 2
                Rh = res("hnT%d" % h3)
                q_a, k_a, q_b, k_b = qk[gs]
                Rqk = res("qk%d" % gs)
                Rdec = res("dec%d" % gs)
                pcs = []
                for t in range(4):
                    def vpe(t=t):
                        def mmv(e):
                            ins = None
                            for c in range(8):
                                ins = e.matmul(p_vg[:, 0:256], lhsT=hnT[h3][:, c, t * 128:(t + 1) * 128],
                                               rhs=wtm_sb[:, c, :], start=(c == 0), stop=(c == 7))
                            return ins
                        P.op("pe", mmv, reads=[Rh, res("wtm")], writes=[res("p_vg")])

                    def vrest(t=t):
                        P.op("dve", lambda e: e.tensor_copy(out=v_sb[gs][:, t, :], in_=p_vg[:, 0:256]),
                             reads=[res("p_vg")], writes=[res("v_sb%d" % gs)])
                    pcs.append((None, (lambda vpe=vpe, vrest=vrest: (vpe(), vrest()))))

                st = {}

                def mk_proj(key, col0, M):
                    def f():
                        st[key] = proj_fm(G, col0, M)
                    return f

                def glr_rest():
                    bank, Rb = st["glr"]
                    fm_release(bank)
                    P.op("act", lambda e: e.copy(out=glrT[0:16, :], in_=bank[0:16, :]),
                         reads=[Rb], writes=[res("glrT")])
                    bank2, Rb2 = fm_bank()
                    st["xg"] = (bank2, Rb2)
                    P.op("pe", lambda e: e.matmul(
                        bank2[0:64, :], lhsT=wgk_sb[0:17, :], rhs=glrT[0:17, :], start=True, stop=True),
                        reads=[res("glrT"), res("small")], writes=[Rb2])
                pcs = [(mk_proj("glr", 640, 16), (lambda: None)), pcs[0], pcs[1], (None, glr_rest), pcs[2], pcs[3]]

                def gla_rest():
                    bank, Rb = st["xg"]
                    fm_release(bank)
                    P.op("act", lambda e: e.activation(
                        out=tA[0][0:64, :], in_=bank[0:64, :], func=AF.Exp, scale=-1.0),
                        reads=[Rb], writes=[res("tA0")])
                    P.op("act", lambda e: e.activation(
                        out=tA[0][0:64, :], in_=tA[0][0:64, :], func=AF.Ln, bias=one_t[0:64, 0:1]),
                        reads=[res("tA0"), R_const], writes=[res("tA0")])
                    P.op("dve", lambda e: e.tensor_tensor_scan(
                        out=tA[1][0:64, :], data0=scanmask[0:64, :], data1=tA[0][0:64, :], initial=0.0,
                        op0=ALU.mult, op1=ALU.add),
                        reads=[res("tA0"), R_const], writes=[res("tA1")])
                    P.op("act", lambda e: e.activation(
                        out=tA[2][0:64, :], in_=tA[1][0:64, :], func=AF.Exp, scale=-1.0 / 16),
                        reads=[res("tA1")], writes=[res("tA2")])
                    P.op("act", lambda e: e.activation(
                        out=tA[3][0:64, :], in_=tA[1][0:64, :], func=AF.Exp, scale=1.0 / 16),
                        reads=[res("tA1")], writes=[res("tA3")])
                    P.op("dve", lambda e: e.tensor_copy(
                        out=dec[gs][0:64, 0:4],
                        in_=tA[2][0:64, :].rearrange("p (c k) -> p c k", k=CH)[:, :, CH - 1]),
                        reads=[res("tA2")], writes=[Rdec])
                pcs.append((None, gla_rest))

                def qa_rest():
                    bank, Rb = st["qa"]
                    fm_release(bank)
                    P.op("dve", lambda e: e.scalar_tensor_tensor(
                        out=q_a[0:64, :], in0=bank[0:64, :], scalar=0.125, in1=tA[2][0:64, :],
                        op0=ALU.mult, op1=ALU.mult),
                        reads=[Rb, res("tA2")], writes=[Rqk])
                pcs.append((mk_proj("qa", 0, 64), qa_rest))

                def ka_rest():
                    bank, Rb = st["ka"]
                    fm_release(bank)
                    P.op("dve", lambda e: e.tensor_mul(
                        out=k_a[0:64, :], in0=bank[0:64, :], in1=tA[3][0:64, :]),
                        reads=[Rb, res("tA3")], writes=[Rqk])
                pcs.append((mk_proj("ka", 64, 64), ka_rest))

                def f1_rest():
                    bank, Rb = st["f"]
                    fm_release(bank)
                    P.op("act", lambda e: e.activation(out=tB[0][:], in_=bank[:], func=AF.Exp),
                         reads=[Rb], writes=[res("tB0")])
                    P.op("act", lambda e: e.activation(
                        out=tB[0][:], in_=tB[0][:], func=AF.Ln, bias=one_t[:, 0:1]),
                        reads=[res("tB0"), R_const], writes=[res("tB0")])
                    P.op("act", lambda e: e.activation(
                        out=tB[2][:], in_=tB[0][:], func=AF.Exp, scale=-1.0),
                        reads=[res("tB0")], writes=[res("tB2")])
                    P.op("act", lambda e: e.activation(
                        out=tB[3][:], in_=tB[2][:], func=AF.Ln, scale=nomlb, bias=one_t[:, 0:1]),
                        reads=[res("tB2"), R_const, res("omlb")], writes=[res("tB3")])
                pcs.append((mk_proj("f", 384, 128), f1_rest))

                def f2_rest():
                    P.op("dve", lambda e: e.tensor_tensor_scan(
                        out=tB[0][:], data0=scanmask[:], data1=tB[3][:], initial=0.0,
                        op0=ALU.mult, op1=ALU.add),
                        reads=[res("tB3"), R_const], writes=[res("tB0")])
                    P.op("act", lambda e: e.activation(out=tB[1][:], in_=tB[0][:], func=AF.Exp, scale=-1.0),
                         reads=[res("tB0")], writes=[res("tB1")])
                    P.op("act", lambda e: e.activation(out=tB[4][:], in_=tB[0][:], func=AF.Exp),
                         reads=[res("tB0")], writes=[res("tB4")])
                    P.op("dve", lambda e: e.tensor_copy(
                        out=dec[gs][:, 8:12],
                        in_=tB[4][:].rearrange("p (c k) -> p c k", k=CH)[:, :, CH - 1]),
                        reads=[res("tB4")], writes=[Rdec])
                    P.op("dve", lambda e: e.scalar_tensor_tensor(
                        out=k_b[:], in0=tB[2][:], scalar=omlb, in1=tB[1][:], op0=ALU.mult, op1=ALU.mult),
                        reads=[res("tB2"), res("tB1"), res("omlb")], writes=[Rqk])
                pcs.append((None, f2_rest))

                def q_rest():
                    bank, Rb = st["q"]
                    fm_release(bank)
                    P.op("act", lambda e: e.activation(out=tB[5][:], in_=bank[:], func=AF.Exp, scale=-1.0),
                         reads=[Rb], writes=[res("tB5")])
                    P.op("act", lambda e: e.activation(
                        out=tB[5][:], in_=tB[5][:], func=AF.Ln, bias=one_t[:, 0:1]),
                        reads=[res("tB5"), R_const], writes=[res("tB5")])
                    P.op("act", lambda e: e.activation(
                        out=tB[5][:], in_=tB[5][:], func=AF.Exp, scale=-1.0),
                        reads=[res("tB5")], writes=[res("tB5")])
                    P.op("dve", lambda e: e.tensor_mul(out=tB[3][:], in0=bank[:], in1=tB[5][:]),
                         reads=[Rb, res("tB5")], writes=[res("tB3")])
                    P.op("dve", lambda e: e.tensor_mul(out=q_b[:], in0=tB[3][:], in1=tB[4][:]),
                         reads=[res("tB3"), res("tB4")], writes=[Rqk])
                pcs.append((mk_proj("q", 256, 128), q_rest))

                for h, col0 in ((0, 128), (1, 512)):
                    def g_rest(h=h):
                        bank, Rb = st["g%d" % h]
                        fm_release(bank)
                        Rc = res("tC0")
                        P.op("act", lambda e: e.activation(out=tC[0][:], in_=bank[:], func=AF.Exp, scale=-1.0),
                             reads=[Rb], writes=[Rc])
                        P.op("act", lambda e: e.activation(
                            out=tC[0][:], in_=tC[0][:], func=AF.Ln, bias=one_t[:, 0:1]),
                            reads=[Rc, R_const], writes=[Rc])
                        P.op("act", lambda e: e.activation(
                            out=tC[0][:], in_=tC[0][:], func=AF.Exp, scale=-1.0),
                            reads=[Rc], writes=[Rc])
                        P.op("dve", lambda e: e.tensor_mul(out=sg[gs][h][:], in0=bank[:], in1=tC[0][:]),
                             reads=[Rb, Rc], writes=[res("sg%d_%d" % (gs, h))])
                    pcs.append((mk_proj("g%d" % h, col0, 128), g_rest))
                return pcs

            def pipeline_b(pcs):
                outl = []
                n = len(pcs)
                if not PE_AHEAD:
                    for i in range(n):
                        def f0(i=i):
                            if pcs[i][0] is not None:
                                pcs[i][0]()
                            pcs[i][1]()
                        outl.append(f0)
                    return outl
                for i in range(n + 1):
                    def f(i=i):
                        if i < n and pcs[i][0] is not None:
                            pcs[i][0]()
                        if i >= 1:
                            pcs[i - 1][1]()
                    outl.append(f)
                return outl

            sb_idx = [0]
            NSB = 4

            def a_pieces(G):
                gs = G % 2
                q_a, k_a, q_b, k_b = qk[gs]
                Rqk, Rdec, Rv = res("qk%d" % gs), res("dec%d" % gs), res("v_sb%d" % gs)
                pcs = []
                for t in range(4):
                    n = G * 4 + t
                    c0, c1 = t * 128, (t + 1) * 128
                    ks = n % 2
                    Rkd, Rscm = res("kd_tok%d" % ks), res("scm%d" % ks)

                    def pre(t=t, c0=c0, c1=c1, ks=ks, Rkd=Rkd, Rscm=Rscm):
                        def trk(e):
                            e.transpose(out=p_kt[:, 0:64], in_=k_a[0:64, c0:c1], identity=ident[0:64, 0:64])
                            return e.transpose(out=p_kt[:, 64:192], in_=k_b[:, c0:c1], identity=ident[:])
                        P.op("pe", trk, reads=[Rqk, R_ident], writes=[res("p_ks")])

                        def sc(e):
                            e.matmul(p_ks[:, 0:128], lhsT=k_a[0:64, c0:c1], rhs=q_a[0:64, c0:c1],
                                     start=True, stop=True)
                            return e.matmul(p_ks[:, 128:256], lhsT=k_b[:, c0:c1], rhs=q_b[:, c0:c1],
                                            start=True, stop=True)
                        P.op("pe", sc, reads=[Rqk], writes=[res("p_ks")])

                        def cpk(e):
                            return e.tensor_copy(out=kd_tok[ks][:, 0:192], in_=p_kt[:, 0:192])
                        P.op("dve", cpk, reads=[res("p_ks")], writes=[Rkd])
                        P.op("dve", lambda e: e.tensor_mul(
                            out=scm[ks][:], in0=p_ks[:, 0:256], in1=cmask[:]),
                            reads=[res("p_ks"), res("small")], writes=[Rscm])

                    def pre2(t=t, ks=ks, Rkd=Rkd):
                        def ds(e):
                            e.matmul(p_st[0:64, 0, :], lhsT=kd_tok[ks][:, 0:64],
                                     rhs=v_sb[gs][:, t, 0:128], start=True, stop=True)
                            return e.matmul(p_st[:, 1, :], lhsT=kd_tok[ks][:, 64:192],
                                            rhs=v_sb[gs][:, t, 128:256], start=True, stop=True)
                        P.op("pe", ds, reads=[Rkd, Rv], writes=[res("p_st")])
                    tile_pcs = [pre, pre2]

                    def chunk(t=t, c0=c0, c1=c1, ks=ks, Rscm=Rscm):
                        si = sb_idx[0] % NSB
                        so = (si + 1) % NSB
                        sb_idx[0] += 1
                        sl = t % 2
                        Rtmp = res("tmpS%d" % sl)

                        def tmp(e):
                            e.activation(out=tmpS[0:64, sl, :], in_=p_st[0:64, 0, :], func=AF.Identity,
                                         scale=dec[gs][0:64, t:t + 1])
                            return e.activation(out=tmpS[:, 2 + sl, :], in_=p_st[:, 1, :], func=AF.Identity,
                                                scale=dec[gs][:, 8 + t:9 + t])
                        P.op("act", tmp, reads=[res("p_st"), Rdec], writes=[Rtmp])

                        def om(e):
                            e.matmul(p_ot[0][:, c0:c1], lhsT=v_sb[gs][:, t, 0:128],
                                     rhs=scm[ks][:, 0:128], start=True, stop=False)
                            e.matmul(p_ot[0][:, c0:c1], lhsT=Sb[si][0:64, 0:128], rhs=q_a[0:64, c0:c1],
                                     start=False, stop=True)
                            e.matmul(p_ot[1][:, c0:c1], lhsT=v_sb[gs][:, t, 128:256],
                                     rhs=scm[ks][:, 128:256], start=True, stop=False)
                            return e.matmul(p_ot[1][:, c0:c1], lhsT=Sb[si][:, 128:256], rhs=q_b[:, c0:c1],
                                            start=False, stop=True)
                        P.op("pe", om, reads=[Rv, Rscm, Rqk, res("Sb%d" % si)],
                             writes=[res("p_ot0"), res("p_ot1")])

                        def upd(e):
                            e.scalar_tensor_tensor(out=S32[0:64, 0:128], in0=S32[0:64, 0:128],
                                                   scalar=dec[gs][0:64, t:t + 1], in1=tmpS[0:64, sl, :],
                                                   op0=ALU.mult, op1=ALU.add)
                            return e.scalar_tensor_tensor(out=S32[:, 128:256], in0=S32[:, 128:256],
                                                          scalar=dec[gs][:, 8 + t:9 + t], in1=tmpS[:, 2 + sl, :],
                                                          op0=ALU.mult, op1=ALU.add)
                        P.op("dve", upd, reads=[Rtmp, Rdec, res("S32")], writes=[res("S32")])
                        P.op("dve", lambda e: e.tensor_copy(out=Sb[so][:], in_=S32[:]),
                             reads=[res("S32")], writes=[res("Sb%d" % so)])
                    tile_pcs.append(chunk)
                    pcs.append(tile_pcs)

                tiles = pcs
                pcs = [tiles[0][0]]
                for t in range(4):
                    if t + 1 < 4:
                        pcs.append(tiles[t + 1][0])
                    pcs += tiles[t][1:]
                ys = G % 2
                q, gq = G // 4, G % 4
                Ry = res("yT0")
                n1b = []
                for h in range(2):
                    def n1a(h=h):
                        P.op("act", lambda e: e.copy(out=tD[h][:], in_=p_ot[h][:]),
                             reads=[res("p_ot%d" % h)], writes=[res("tD%d" % h)])

                    def n1(h=h):
                        Rt = res("tD%d" % h)
                        P.op("act", lambda e: e.activation(out=osq[:], in_=tD[h][:], func=AF.Square),
                             reads=[Rt], writes=[res("osq")])
                        bank, Rb = fm_bank()
                        P.op("pe", lambda e: e.matmul(
                            bank[:], lhsT=ones_bf[:], rhs=osq[:], start=True, stop=True),
                            reads=[res("osq"), R_const], writes=[Rb])
                        P.op("act", lambda e: e.activation(
                            out=tD[2][:], in_=bank[:], func=AF.Ln, scale=1.0 / 128, bias=eps_t[:, 0:1]),
                            reads=[Rb, R_const], writes=[res("tD2")])
                        P.op("act", lambda e: e.activation(out=tD[2][:], in_=tD[2][:], func=AF.Exp, scale=-0.5),
                             reads=[res("tD2")], writes=[res("tD2")])
                        P.op("dve", lambda e: e.tensor_mul(out=tD[2][:], in0=tD[h][:], in1=tD[2][:]),
                             reads=[Rt, res("tD2")], writes=[res("tD2")])
                        P.op("dve", lambda e: e.scalar_tensor_tensor(
                            out=yT[0][:, h, :], in0=tD[2][:], scalar=gnw_sb[:, h:h + 1], in1=sg[ys][h][:],
                            op0=ALU.mult, op1=ALU.mult),
                            reads=[res("tD2"), res("sg%d_%d" % (ys, h)), res("small")], writes=[Ry])
                    pcs.append(n1a)
                    n1b.append(n1)

                def store():
                    P.dma("sp", lambda e, then: then(e.dma_start(
                        out=ybounce[q, :, gq * GT:(gq + 1) * GT].rearrange("(h p) n -> p h n", p=128),
                        in_=yT[0][:])), Ry, reads=[Ry], writes=[R_yb[q]])
                    if gq == 3:
                        P.collective(lambda e: e.collective_compute(
                            "AllGather", ALU.bypass, replica_groups=[[0, 1, 2, 3], [4, 5, 6, 7]],
                            ins=[ybounce[q]], outs=[agout[q]]),
                            reads=[R_yb[q]], writes=[R_ag[q]])
                return pcs, n1b + [store]

            def interleave(main, bulk, deferred=None):
                nm, nb = len(main), len(bulk)
                j = 0
                for i, m in enumerate(main):
                    m()
                    if deferred and i in DEFER_AT:
                        deferred[DEFER_AT.index(i)]()
                    tgt = ((i + 1) * nb) // nm
                    while j < tgt:
                        bulk[j]()
                        j += 1
                while j < nb:
                    bulk[j]()
                    j += 1

            def merge(b, c):
                outl = []
                nb, ncc = len(b), len(c)
                j = 0
                for i, f in enumerate(b):
                    outl.append(f)
                    tgt = ((i + 1) * ncc) // max(nb, 1)
                    while j < tgt:
                        outl.append(c[j])
                        j += 1
                outl += c[j:]
                return outl

            EARLY_X = True
            for n in range(1, 4 if EARLY_X else NXP):
                x_load(n)
            for f in x_pieces(0):
                f()
            if EARLY_X:
                for n in range(4, 8):
                    x_load(n)
            for f in merge(pipeline_b(e_pieces(0)), x_pieces(1)):
                f()
            pid_m = [None]

            def prefetch_y0():
                def ld_y0(e, then):
                    if pid_m[0] is None:
                        pid_m[0] = e.partition_id()
                    src = agout[0].rearrange("f (s n) -> s f n", n=GT)[bass.ds(pid_m[0] % 4, 1), :, :]
                    then(e.dma_start(out=yin0[:], in_=src.rearrange("a (j p) n -> p (a j) n", p=128)))
                P.dma("pool", ld_y0, R_yin0, reads=[R_ag[0]], writes=[R_yin0])

            pending = None
            for G in range(NG_MIX):
                big_loads(G)
                if G == 9:
                    prefetch_y0()
                bulk_b = pipeline_b(e_pieces(G + 1)) if G + 1 < NG_MIX else []
                bulk_c = x_pieces(G + 2) if G + 2 < NG_MIX else []
                if EARLY_X and G + 2 < NG_MIX:
                    for n in range(4 * (G + 2), 4 * (G + 2) + 4):
                        x_load(n)
                apcs, tail = a_pieces(G)
                interleave(apcs, merge(bulk_b, bulk_c), pending)
                pending = tail
            for f in pending:
                f()

            P.finalize()
            with nc.Block() as block:
                @block.sync
                def _(e):
                    P.run("sp", e)

                @block.scalar
                def _(e):
                    P.run("act", e)

                @block.vector
                def _(e):
                    P.run("dve", e)

                @block.gpsimd
                def _(e):
                    P.run("pool", e)

                @block.tensor
                def _(e):
                    P.run("pe", e)

        for r_ in [R_wout, R_wdown, R_ident, R_const, R_wupb, R_yin0] + R_yb + R_ag:
            r_.ws = [ev for ev in r_.ws if ev.dma]
            r_.rs = [ev for ev in r_.rs if ev.dma]
        with contextlib.ExitStack() as es:
            nbc = [sb(es, "nbc%d" % i, [128, D]) for i in range(3)]
            wupc = [sb(es, "wupc%d" % i, [128, 8, 512], BF16) for i in range(2)]
            yin = [yin0, sb(es, "yin1", [128, 8, GT], BF16)]
            NH = 6
            h1 = [sb(es, "h1_%d" % i, [128, D]) for i in range(NH)]
            fst = [sb(es, "fst%d" % i, [128, 8]) for i in range(NH)]
            junk2 = sb(es, "junk2", [128, D], BF16)
            hn1b = [sb(es, "hn1b%d" % i, [128, D], BF16) for i in range(2)]
            hn1T = sb(es, "hn1T", [128, 8, GT], BF16)
            aT = sb(es, "aT", [128, 32, GT], BF16)
            rl = [sb(es, "rl%d" % i, [128, GT]) for i in range(2)]
            mtmp = [sb(es, "mtmp%d" % i, [128, D]) for i in range(2)]

            q_tr = ps(es, "q_tr", [128, 8, 128], BF16)
            q_mix = ps(es, "q_mix", [128, D])
            q_up = [ps(es, "q_up%d" % i, [128, GT]) for i in range(3)]
            q_dn = ps(es, "q_dn", [128, D])

            R = {"yin0": R_yin0}

            def res(name):
                if name not in R:
                    R[name] = Res(name)
                return R[name]

            def ld_nbc(i):
                P.dma("sp", lambda e, then: then(e.dma_start(
                    out=nbc[i][:], in_=nrm[i + 1:i + 2, :].partition_broadcast(128))),
                    res("nbc%d" % i), writes=[res("nbc%d" % i)])

            pid = [None]
            wu_n = [0]

            def load_wup(fc4):
                s = wu_n[0] % 2
                wu_n[0] += 1
                Rw = res("wupc%d" % s)
                P.dma("sp", lambda e, then, s=s, fc4=fc4: then(e.dma_start(
                    out=wupc[s][:], in_=wupb[:, fc4 * 512:(fc4 + 1) * 512].rearrange("(c p) n -> p c n", p=128))),
                    Rw, reads=[R_wupb], writes=[Rw])
                return s, Rw

            def rms_from(src_ap, src_res, st, col, junk_res):
                Rst = res("fst%d" % st)
                P.op("act", lambda e: e.activation(out=junk2[:], in_=src_ap, func=AF.Square,
                                                   accum_out=fst[st][:, col:col + 1]),
                     reads=[src_res], writes=[junk_res, Rst])
                P.op("act", lambda e: e.activation(out=fst[st][:, col + 1:col + 2], in_=fst[st][:, col:col + 1],
                                                   func=AF.Ln, scale=1.0 / D, bias=eps_t[:, 0:1]),
                     reads=[Rst, R_const], writes=[Rst])
                P.op("act", lambda e: e.activation(out=fst[st][:, col + 2:col + 3], in_=fst[st][:, col + 1:col + 2],
                                                   func=AF.Exp, scale=-0.5),
                     reads=[Rst], writes=[Rst])
                return Rst

            out_evs = []

            def load_y(g):
                ysl = g % 2
                Ryin = res("yin%d" % ysl)

                def ld_y(e, then, g=g, ysl=ysl):
                    if pid[0] is None:
                        pid[0] = e.partition_id()
                    src = agout[g].rearrange("f (s n) -> s f n", n=GT)[bass.ds(pid[0] % 4, 1), :, :]
                    then(e.dma_start(out=yin[ysl][:], in_=src.rearrange("a (j p) n -> p (a j) n", p=128)))
                P.dma("pool", ld_y, Ryin, reads=[R_ag[g]], writes=[Ryin])

            def front_x(g, t):
                n = g * 4 + t
                hs = n % NH
                Rh1 = res("h1_%d" % hs)
                row0 = g * GT + t * 128
                P.dma("sp", lambda e, then: then(
                    e.dma_start(out=h1[hs][:], in_=xo[row0:row0 + 128, :])), Rh1, writes=[Rh1])

            def front_a(g, t, load=True):
                ysl = g % 2
                Ryin = res("yin%d" % ysl)
                n = g * 4 + t
                hs = n % NH
                Rh1 = res("h1_%d" % hs)
                if load:
                    front_x(g, t)

                def mmix(e):
                    ins = None
                    for half in range(2):
                        for j in range(8):
                            c = (j % 2) * 4 + (j // 2)
                            ins = e.matmul(q_mix[:, half * 512:(half + 1) * 512],
                                           lhsT=yin[ysl][:, j, t * 128:(t + 1) * 128],
                                           rhs=w_out_sb[:, c, half * 512:(half + 1) * 512],
                                           start=(j == 0), stop=(j == 7))
                    return ins
                P.op("pe", mmix, reads=[Ryin, R_wout], writes=[res("q_mix")])
                Rst = rms_from(q_mix[:], res("q_mix"), hs, 0, res("junk2"))
                ms = n % 2
                Rm = res("mtmp%d" % ms)
                P.op("dve", lambda e: e.scalar_tensor_tensor(
                    out=mtmp[ms][:], in0=q_mix[:], scalar=fst[hs][:, 2:3], in1=nbc[0][:],
                    op0=ALU.mult, op1=ALU.mult),
                    reads=[res("q_mix"), Rst, res("nbc0")], writes=[Rm])
                P.op("dve", lambda e: e.tensor_add(out=h1[hs][:], in0=h1[hs][:], in1=mtmp[ms][:]),
                     reads=[Rm, Rh1], writes=[Rh1])
                Rst = rms_from(h1[hs][:], Rh1, hs, 3, res("junk2"))
                hb = n % 2
                Rhb = res("hn1b%d" % hb)
                P.op("dve", lambda e: e.scalar_tensor_tensor(
                    out=hn1b[hb][:], in0=h1[hs][:], scalar=fst[hs][:, 5:6], in1=nbc[1][:],
                    op0=ALU.mult, op1=ALU.mult),
                    reads=[Rh1, Rst, res("nbc1")], writes=[Rhb])


            def front_b(g, t):
                n = g * 4 + t
                hb = n % 2
                Rhb = res("hn1b%d" % hb)

                def tr(e):
                    ins = None
                    for c in range(8):
                        ins = e.transpose(out=q_tr[:, c, :], in_=hn1b[hb][:, c * 128:(c + 1) * 128],
                                          identity=ident[:])
                    return ins
                P.op("pe", tr, reads=[Rhb, R_ident], writes=[res("q_tr")])
                P.op("act", lambda e: e.copy(out=hn1T[:, :, t * 128:(t + 1) * 128], in_=q_tr[:]),
                     reads=[res("q_tr")], writes=[res("hn1T")])

            wup_pre = {}

            def prefetch_wup(g):
                wup_pre[g] = [load_wup(0), load_wup(1)]

            def up(g, pipe):
                for fc4 in range(8):
                    if pipe and fc4 == 7 and g + 1 < NG_FFN:
                        front_a(g + 1, 0)
                    if g in wup_pre and fc4 < 2:
                        s, Rw = wup_pre[g][fc4]
                    else:
                        s, Rw = load_wup(fc4)
                    for i in range(4):
                        fc = fc4 * 4 + i
                        ub = fc % 3
                        Rub = res("q_up%d" % ub)

                        def mup(e, s=s, i=i, ub=ub):
                            ins = None
                            for k in range(8):
                                ins = e.matmul(q_up[ub][:], lhsT=wupc[s][:, k, i * 128:(i + 1) * 128],
                                               rhs=hn1T[:, k, :], start=(k == 0), stop=(k == 7))
                            return ins
                        P.op("pe", mup, reads=[Rw, res("hn1T")], writes=[Rub])
                        rs = fc % 2
                        Rrl = res("rl%d" % rs)
                        P.op("act", lambda e, ub=ub, rs=rs: e.activation(out=rl[rs][:], in_=q_up[ub][:], func=AF.Relu),
                             reads=[Rub], writes=[Rrl])
                        P.op("dve", lambda e, rs=rs, fc=fc: e.tensor_mul(out=aT[:, fc, :], in0=rl[rs][:], in1=rl[rs][:]),
                             reads=[Rrl], writes=[res("aT")])

            def down(g, t):
                n = g * 4 + t
                hs = n % NH
                Rh1 = res("h1_%d" % hs)
                alt = (g == NG_FFN - 1) and (t % 2 == 1)
                q_acc = q_mix if alt else q_dn
                Racc = res("q_mix") if alt else res("q_dn")

                def mdn(e):
                    ins = None
                    for half in range(2):
                        for fc in range(32):
                            ins = e.matmul(q_acc[:, half * 512:(half + 1) * 512],
                                           lhsT=aT[:, fc, t * 128:(t + 1) * 128],
                                           rhs=w_down_sb[:, fc, half * 512:(half + 1) * 512],
                                           start=(fc == 0), stop=(fc == 31))
                    return ins
                P.op("pe", mdn, reads=[res("aT"), R_wdown], writes=[Racc])
                Rst = rms_from(q_acc[:], Racc, hs, 0, res("junk2"))
                ms = n % 2
                Rm = res("mtmp%d" % ms)
                P.op("dve", lambda e: e.scalar_tensor_tensor(
                    out=mtmp[ms][:], in0=q_acc[:], scalar=fst[hs][:, 2:3], in1=nbc[2][:],
                    op0=ALU.mult, op1=ALU.mult),
                    reads=[Racc, Rst, res("nbc2")], writes=[Rm])
                P.op("dve", lambda e: e.tensor_add(out=h1[hs][:], in0=h1[hs][:], in1=mtmp[ms][:]),
                     reads=[Rm, Rh1], writes=[Rh1])
                row0 = g * GT + t * 128
                ev = P.dma("sp", lambda e, then: then(
                    e.dma_start(out=out[row0:row0 + 128, :], in_=h1[hs][:])), Rh1, reads=[Rh1])
                out_evs.append(ev)

            FFN_PIPE = True
            front_x(0, 0)
            ld_nbc(0)
            ld_nbc(1)
            ld_nbc(2)
            front_a(0, 0, load=False)
            for t in range(4):
                if t + 1 < 4:
                    front_a(0, t + 1)
                if t == 0:
                    prefetch_wup(0)
                    load_y(1)
                front_b(0, t)
            for g in range(NG_FFN):
                up(g, FFN_PIPE)
                nxt = g + 1 < NG_FFN
                if nxt:
                    prefetch_wup(g + 1)
                if g + 2 < NG_FFN:
                    load_y(g + 2)
                for t in range(4):
                    down(g, t)
                    if nxt and FFN_PIPE:
                        front_b(g + 1, t)
                        if t + 1 < 4:
                            front_a(g + 1, t + 1)
                    elif nxt:
                        front_a(g + 1, t)
                        front_b(g + 1, t)

            P.wait_all("sp", out_evs)
            P.finalize()
            with nc.Block() as block:
                @block.sync
                def _(e):
                    P.run("sp", e)

                @block.scalar
                def _(e):
                    P.run("act", e)

                @block.vector
                def _(e):
                    P.run("dve", e)

                @block.gpsimd
                def _(e):
                    P.run("pool", e)

                @block.tensor
                def _(e):
                    P.run("pe", e)
    return nc


_NC_CACHE = {}


def kernel(x, w_in, w_gk_up, b_gk, gla_norm_w, hgrn_norm_w, hgrn_lower_bounds, w_out,
           pre_mix_norm, post_mix_norm, pre_mlp_norm, post_mlp_norm, w_up, w_down):
    f = lambda a: np.ascontiguousarray(np.asarray(a, dtype=np.float32))
    x = f(x)
    w_in0 = f(w_in)[0]
    o_gq, o_gk, o_gv, o_gg, o_glr, o_hq, o_hf, o_hi, o_hg = 0, 256, 512, 1024, 1536, 1552, 2064, 2576, 3088
    nrm = f(np.stack([np.asarray(pre_mix_norm)[0], np.asarray(post_mix_norm)[0],
                      np.asarray(pre_mlp_norm)[0], np.asarray(post_mlp_norm)[0]], 0))
    jj, ii = np.meshgrid(np.arange(128), np.arange(128), indexing="ij")
    cm = ((jj <= ii) & (jj // CH == ii // CH)).astype(np.float32)
    cmask = f(np.concatenate([cm, cm], 1))
    ident = np.eye(128, dtype=np.float32)
    w_out0, w_up0, w_down0 = f(w_out)[0], f(w_up)[0], f(w_down)[0]
    in_maps = []
    for c in range(8):
        b, hg = c // 4, c % 4
        a64 = slice(hg * 64, hg * 64 + 64)
        a128 = slice(hg * 128, hg * 128 + 128)
        cols = lambda o, s: w_in0[:, o + s.start:o + s.stop]
        wfm = np.concatenate([cols(o_gq, a64), cols(o_gk, a64), cols(o_gg, a128), cols(o_hq, a128),
                              cols(o_hf, a128), cols(o_hg, a128), w_in0[:, o_glr:o_glr + 16]], 1)
        wtm = np.concatenate([cols(o_gv, a128), cols(o_hi, a128)], 1)
        wgk = np.concatenate([f(w_gk_up)[0][:, a64], f(b_gk)[0][None, a64]], 0)
        lbp = f(hgrn_lower_bounds)[:, a128].T
        gnw = np.stack([f(gla_norm_w)[0][a128], f(hgrn_norm_w)[0][a128]], 1)
        in_maps.append({
            "xb": x[b],
            "xo": f(np.concatenate([x[b, q * TOK + hg * GT:q * TOK + (hg + 1) * GT] for q in range(4)], 0)),
            "wfm": f(wfm), "wtm": f(wtm), "wgk": f(wgk), "lbp": f(lbp), "gnw": f(gnw), "nrm": nrm,
            "w_out": w_out0, "w_up": w_up0, "w_down": w_down0, "cmask": cmask, "ident": ident,
        })
    if "nc" not in _NC_CACHE:
        _NC_CACHE["nc"] = build_nc()
    res = run_bass_kernel_spmd(_NC_CACHE["nc"], in_maps, core_ids=list(range(8)))
    outp = np.empty((2, SEQ, D), np.float32)
    for c in range(8):
        b, s = c // 4, c % 4
        o_c = res.results[c]["out"]
        for q in range(4):
            outp[b, q * TOK + s * GT:q * TOK + (s + 1) * GT] = o_c[q * GT:(q + 1) * GT]
    return outp
```
